# Optimizing a Trainium2 kernel written in Bass

```python
import math
import jax, jax.numpy as jnp
from jax import lax
import numpy as np

D_MODEL = 2048
BATCH = 4
SEQ = 2048
DEPTH = 1

D_MIX = D_MODEL
ATTN_WIDTH = D_MIX // 2
GMLP_WIDTH = D_MIX - ATTN_WIDTH
ATTN_HEAD_DIM = 128
N_ATTN_HEADS = ATTN_WIDTH // ATTN_HEAD_DIM
DIFF_HALF = ATTN_HEAD_DIM // 2
ROPE_DIM = DIFF_HALF // 4
ROPE_THETA = 500000.0
N_GMLP_HEADS = 8
GMLP_HEAD_DIM = GMLP_WIDTH // N_GMLP_HEADS
CHUNK = 128
Q_BLOCK = 128
QK_WIDTH = N_ATTN_HEADS * 2 * DIFF_HALF
IN_WIDTH = 2 * QK_WIDTH + ATTN_WIDTH + 2 * GMLP_WIDTH
D_FF = 5632
CONV_WIDTH = 3
PLE_DIM = 256
EPS = 1e-6

kernel_name = "hybrid_diffattn_gmlp_convffn_encoder"


def rms_norm(x, g):
    xf = x.astype(jnp.float32)
    y = xf * lax.rsqrt(jnp.mean(xf * xf, axis=-1, keepdims=True) + EPS)
    return (y * g.astype(jnp.float32)).astype(x.dtype)


def layer_norm(x, g, b):
    xf = x.astype(jnp.float32)
    mu = jnp.mean(xf, axis=-1, keepdims=True)
    var = jnp.mean(jnp.square(xf - mu), axis=-1, keepdims=True)
    y = (xf - mu) * lax.rsqrt(var + EPS)
    return (y * g.astype(jnp.float32) + b.astype(jnp.float32)).astype(x.dtype)


def rope_tables(positions, dtype):
    inv_freq = ROPE_THETA ** (-jnp.arange(0, ROPE_DIM, 2, dtype=jnp.float32) / ROPE_DIM)
    ang = positions.astype(jnp.float32)[..., None] * inv_freq
    cos = jnp.cos(ang)[:, :, None, None, :].astype(dtype)
    sin = jnp.sin(ang)[:, :, None, None, :].astype(dtype)
    return cos, sin


def apply_partial_rope(t, cos, sin):
    half = ROPE_DIM // 2
    r1 = t[..., :half]
    r2 = t[..., half:ROPE_DIM]
    rotated = jnp.concatenate([r1 * cos - r2 * sin, r2 * cos + r1 * sin], axis=-1)
    return jnp.concatenate([rotated, t[..., ROPE_DIM:]], axis=-1)


def diff_attention(q, k, v, lam, lambda_init, g_subln):
    B, S = q.shape[0], q.shape[1]
    nb = S // Q_BLOCK
    scale = DIFF_HALF ** -0.5
    qb = q.reshape(B, nb, Q_BLOCK, N_ATTN_HEADS, 2, DIFF_HALF).transpose(1, 0, 2, 3, 4, 5)

    def block(qi):
        s = jnp.einsum('bqhcd,bkhcd->bhcqk', qi, k).astype(jnp.float32) * scale
        pr = jax.nn.softmax(s, axis=-1)
        a = pr[:, :, 0] - lam * pr[:, :, 1]
        return jnp.einsum('bhqk,bkhd->bqhd', a.astype(v.dtype), v)

    o = lax.map(block, qb)
    o = o.transpose(1, 0, 2, 3, 4).reshape(B, S, N_ATTN_HEADS, ATTN_HEAD_DIM)
    o = rms_norm(o, g_subln) * (1.0 - lambda_init)
    return o.reshape(B, S, ATTN_WIDTH)


def spatial_gating(u, v, ln_g, ln_b, w_s, b_s):
    B, S = u.shape[0], u.shape[1]
    u = jax.nn.gelu(u, approximate=False)
    v = layer_norm(jax.nn.gelu(v, approximate=False), ln_g, ln_b)
    vc = v.reshape(B, S // CHUNK, CHUNK, N_GMLP_HEADS, GMLP_HEAD_DIM)
    mixed = jnp.einsum('hpq,bcqhd->bcphd', w_s, vc) + b_s.T[None, None, :, :, None]
    return u * mixed.reshape(B, S, GMLP_WIDTH)


def conv_glu_ffn(x, w_up, conv_w, conv_b, w_down):
    h = x @ w_up
    hp = jnp.pad(h, ((0, 0), (1, 1), (0, 0)))
    h = hp[:, :-2] * conv_w[0] + hp[:, 1:-1] * conv_w[1] + hp[:, 2:] * conv_w[2] + conv_b
    g, u = jnp.split(h, 2, axis=-1)
    return (jax.nn.silu(g) * u) @ w_down


def setup_inputs(seed: int = 0) -> dict:
    key = jax.random.key(seed)
    ks = jax.random.split(key, 24)
    f32 = jnp.float32
    nrm = lambda k, shape, s: jax.random.normal(k, shape, f32) * s
    gain = lambda k, shape: 1.0 + 0.05 * jax.random.normal(k, shape, f32)
    L = DEPTH
    x = jax.random.normal(ks[0], (BATCH, SEQ, D_MODEL), f32)
    p = jax.random.normal(ks[1], (DEPTH, BATCH, SEQ, PLE_DIM), f32)
    positions = jnp.broadcast_to(jnp.arange(SEQ, dtype=jnp.int32), (BATCH, SEQ))
    return {
        "x": x,
        "p": p,
        "positions": positions,
        "g_mix": gain(ks[2], (L, D_MODEL)),
        "w_in": nrm(ks[3], (L, D_MODEL, IN_WIDTH), D_MODEL ** -0.5),
        "lambda_q1": nrm(ks[4], (L, DIFF_HALF), 0.1),
        "lambda_k1": nrm(ks[5], (L, DIFF_HALF), 0.1),
        "lambda_q2": nrm(ks[6], (L, DIFF_HALF), 0.1),
        "lambda_k2": nrm(ks[7], (L, DIFF_HALF), 0.1),
        "g_subln": gain(ks[8], (L, ATTN_HEAD_DIM)),
        "gmlp_ln_g": gain(ks[9], (L, GMLP_WIDTH)),
        "gmlp_ln_b": nrm(ks[10], (L, GMLP_WIDTH), 0.02),
        "w_spatial": nrm(ks[11], (L, N_GMLP_HEADS, CHUNK, CHUNK), CHUNK ** -0.5),
        "b_spatial": gain(ks[12], (L, N_GMLP_HEADS, CHUNK)),
        "w_out": nrm(ks[13], (L, D_MIX, D_MODEL), D_MIX ** -0.5),
        "g_ffn": gain(ks[14], (L, D_MODEL)),
        "w_up": nrm(ks[15], (L, D_MODEL, 2 * D_FF), D_MODEL ** -0.5),
        "conv_w": nrm(ks[16], (L, CONV_WIDTH, 2 * D_FF), CONV_WIDTH ** -0.5),
        "conv_b": nrm(ks[17], (L, 2 * D_FF), 0.02),
        "w_down": nrm(ks[18], (L, D_FF, D_MODEL), D_FF ** -0.5),
        "g_ple": gain(ks[19], (L, D_MODEL)),
        "w_ple_gate": nrm(ks[20], (L, D_MODEL, D_MODEL), D_MODEL ** -0.5),
        "w_ple_up": nrm(ks[21], (L, PLE_DIM, D_MODEL), PLE_DIM ** -0.5),
        "g_final": gain(ks[22], (D_MODEL,)),
    }


def reference(x, p, positions, g_mix, w_in, lambda_q1, lambda_k1, lambda_q2, lambda_k2,
              g_subln, gmlp_ln_g, gmlp_ln_b, w_spatial, b_spatial, w_out, g_ffn,
              w_up, conv_w, conv_b, w_down, g_ple, w_ple_gate, w_ple_up, g_final):
    B, S = x.shape[0], x.shape[1]
    cos, sin = rope_tables(positions, x.dtype)
    h = x
    for i in range(DEPTH):
        a = rms_norm(h, g_mix[i])
        z = a @ w_in[i]
        o0, o1, o2, o3 = QK_WIDTH, 2 * QK_WIDTH, 2 * QK_WIDTH + ATTN_WIDTH, 2 * QK_WIDTH + ATTN_WIDTH + GMLP_WIDTH
        q = z[..., :o0].reshape(B, S, N_ATTN_HEADS, 2, DIFF_HALF)
        k = z[..., o0:o1].reshape(B, S, N_ATTN_HEADS, 2, DIFF_HALF)
        v = z[..., o1:o2].reshape(B, S, N_ATTN_HEADS, ATTN_HEAD_DIM)
        gu = z[..., o2:o3]
        gv = z[..., o3:]
        q = apply_partial_rope(q, cos, sin)
        k = apply_partial_rope(k, cos, sin)
        lambda_init = 0.8 - 0.6 * math.exp(-0.3 * i)
        lam = (jnp.exp(jnp.sum(lambda_q1[i].astype(jnp.float32) * lambda_k1[i].astype(jnp.float32)))
               - jnp.exp(jnp.sum(lambda_q2[i].astype(jnp.float32) * lambda_k2[i].astype(jnp.float32)))
               + lambda_init)
        attn_out = diff_attention(q, k, v, lam, lambda_init, g_subln[i])
        gmlp_out = spatial_gating(gu, gv, gmlp_ln_g[i], gmlp_ln_b[i],
                                  w_spatial[i], b_spatial[i])
        mix = jnp.concatenate([attn_out, gmlp_out], axis=-1)
        h = h + mix @ w_out[i]
        h = h + conv_glu_ffn(rms_norm(h, g_ffn[i]), w_up[i], conv_w[i], conv_b[i], w_down[i])
        gate = jax.nn.sigmoid(rms_norm(h, g_ple[i]) @ w_ple_gate[i])
        h = h + (p[i] @ w_ple_up[i]) * gate
    return rms_norm(h, g_final)
```

```python
import math
import types
from contextlib import ExitStack
import numpy as np
import concourse.bass as bass
import concourse.mybir as mybir
from concourse.bass_utils import run_bass_kernel_spmd

F32 = mybir.dt.float32
BF16 = mybir.dt.bfloat16
I32 = mybir.dt.int32
AF = mybir.ActivationFunctionType
ALU = mybir.AluOpType
AX = mybir.AxisListType

D = 2048
S = 2048
NOWN = 1024
DFF = 5632
EPS = 1e-6
EPOCH = 6000
TWO_PI = 2.0 * math.pi


def _freeze(fn):
    if fn is None or fn.__closure__ is None:
        return fn
    cells = []
    for c in fn.__closure__:
        try:
            cells.append(types.CellType(c.cell_contents))
        except ValueError:
            cells.append(c)
    return types.FunctionType(fn.__code__, fn.__globals__, fn.__name__, fn.__defaults__, tuple(cells))


class Reg:
    __slots__ = ("w", "r", "name")

    def __init__(self, name=""):
        self.w = None
        self.r = {}
        self.name = name


class DSem:
    def __init__(self, sem):
        self.sem = sem
        self.cnt = 0


class Eng:
    def __init__(self, name):
        self.name = name
        self.items = []
        self.sem = None
        self.cnt = 0
        self.seen = {}
        self.pending = False
        self.owned = set()


class Plan:
    def __init__(self, sems):
        self.pool = list(sems)
        self.E = {n: Eng(n) for n in ("pe", "act", "dve", "pool", "sp")}
        self.dsems = []
        self.skip = False

    def newsem(self):
        return self.pool.pop()

    def new_dsem(self):
        d = DSem(self.newsem())
        self.dsems.append(d)
        return d

    def _waits(self, e, reads, writes):
        deps = {}

        def add(tok):
            if tok is None:
                return
            sem, val = tok
            k = id(sem)
            if k in deps:
                if deps[k][1] < val:
                    deps[k] = (sem, val)
            else:
                deps[k] = (sem, val)

        for r in reads:
            add(r.w)
        for w in writes:
            add(w.w)
            for tok in w.r.values():
                add(tok)
        waits = []
        for k, (sem, val) in deps.items():
            if e.name == "pe" and k in e.owned:
                continue
            if e.seen.get(k, 0) >= val:
                continue
            e.seen[k] = val
            waits.append((sem, val))
        return waits

    def _mark(self, tok, reads, writes):
        k = id(tok[0])
        for r in reads:
            old = r.r.get(k)
            if old is None or old[1] < tok[1]:
                r.r[k] = tok
        for w in writes:
            w.w = tok
            w.r = {}

    def op(self, en, fn, reads=(), writes=(), inc=True):
        if self.skip:
            return
        e = self.E[en]
        if e.sem is None or (e.cnt >= EPOCH and not e.pending):
            e.sem = self.newsem()
            e.owned.add(id(e.sem))
            e.cnt = 0
        waits = self._waits(e, reads, writes)
        tok = (e.sem, e.cnt + 1)
        e.items.append((waits, _freeze(fn), ("inc", e.sem) if inc else None))
        if inc:
            e.cnt += 1
            e.pending = False
        else:
            e.pending = True
        self._mark(tok, reads, writes)

    def dma(self, en, fn, dsem, reads=(), writes=()):
        if self.skip:
            return
        e = self.E[en]
        waits = self._waits(e, reads, writes)
        dsem.cnt += 16
        tok = (dsem.sem, dsem.cnt)
        e.items.append((waits, _freeze(fn), ("dma", dsem.sem)))
        self._mark(tok, reads, writes)

    def barrier(self, engines=("pe", "act", "dve", "pool", "sp")):
        toks = []
        for n, e in self.E.items():
            assert not e.pending
            if e.sem is not None and e.cnt > 0:
                toks.append((e.sem, e.cnt))
        for d in self.dsems:
            if d.cnt > 0:
                toks.append((d.sem, d.cnt))
        for n in engines:
            e = self.E[n]
            waits = []
            for sem, val in toks:
                k = id(sem)
                if n == "pe" and k in e.owned:
                    continue
                if e.seen.get(k, 0) >= val:
                    continue
                e.seen[k] = val
                waits.append((sem, val))
            if waits:
                e.items.append((waits, None, None))

    def replay(self, en, eng):
        for waits, fn, post in self.E[en].items:
            for sem, val in waits:
                eng.wait_ge(sem, val)
            if fn is None:
                continue
            ins = fn(eng)
            if post is not None:
                ins.then_inc(post[1], 1 if post[0] == "inc" else 16)


def build_nc(upto=99, dbg=()):
    nc = bass.Bass("TRN2", target_bir_lowering=False, dynamic_dma_scratch_size=4096)
    dr = {}

    def din(name, shape, dt=F32):
        dr[name] = nc.dram_tensor(name, list(shape), dt, kind="ExternalInput").ap()
        return dr[name]

    x_d = din("x", [S, D])
    pp_d = din("pp", [NOWN, 256])
    pos_d = din("pos", [128, 16], I32)
    invf_d = din("invf", [128, 8])
    gcols_d = din("gcols", [128, 3, 16])
    gfin_d = din("gfin", [128, D])
    grep_d = din("grep", [3, 128, D])
    lnrep_d = din("lnrep", [128, 2, 1024])
    gsub_d = din("gsub", [128, 128])
    lamv_d = din("lamv", [128, 4, 64])
    wsT_d = din("wsT", [128, 8, 128])
    bsT_d = din("bsT", [128, 8])
    cw_d = din("cw", [128, 88, 4])
    w_in_d = din("w_in", [D, 5120])
    w_out_d = din("w_out", [D, D])
    w_up_d = din("w_up", [D, 2 * DFF])
    w_down_d = din("w_down", [DFF, D])
    w_gate_d = din("w_gate", [D, D])
    w_pleup_d = din("w_pleup", [256, D])
    y_d = nc.dram_tensor("y", [NOWN, D], F32, kind="ExternalOutput").ap()
    dbg_d = {}
    dbg_shapes = {"aT": ([128, 16 * 1152], BF16), "qT": ([128, 8 * 1026], BF16), "kT": ([128, 8 * 2048], BF16),
                  "Vext": ([128, 16 * 8 * 130], BF16), "gu": ([128, 9 * 1024], BF16), "vn": ([128, 9 * 1024], BF16),
                  "mix": ([128, 16 * 1026], BF16), "resid": ([128, 9 * 2048], F32), "hnT": ([128, 16 * 1026], BF16),
                  "cs": ([128, 256], F32)}
    for nm in dbg:
        shp, dt = dbg_shapes[nm]
        dbg_d[nm] = nc.dram_tensor("dbg_" + nm, shp, dt, kind="ExternalOutput").ap()

    TOTAL = 225000 // 4
    with ExitStack() as es:
        big = es.enter_context(nc.sbuf_tensor("big", [128, TOTAL], F32))
        banks = [es.enter_context(nc.psum_tensor("ps%d" % i, [128, 512], F32)) for i in range(8)]
        sems = [es.enter_context(nc.semaphore("s%d" % i)) for i in range(100)]
        P = Plan(sems)
        off = [0]

        def carve(nbytes):
            o = off[0]
            off[0] += (nbytes + 31) // 32 * 32
            assert off[0] <= TOTAL * 4, off[0]
            return o

        def V(o, dt, *dims, parts=128):
            n = 1
            for d_ in dims:
                n *= d_
            if dt == F32 or dt == I32:
                ap = big[0:parts, o // 4:o // 4 + n]
                if dt == I32:
                    ap = ap.bitcast(I32)
            else:
                ap = big[0:parts, o // 4:o // 4 + (n + 1) // 2].bitcast(BF16)
                if n % 2:
                    ap = ap[:, 0:n]
            if len(dims) == 2:
                ap = ap.rearrange("p (a b) -> p a b", a=dims[0])
            elif len(dims) == 3:
                ap = ap.rearrange("p (a b c) -> p a b c", a=dims[0], b=dims[1])
            return ap

        Z1 = carve(36864)
        Z2 = carve(82560)
        Z3 = carve(36864)
        Z4 = carve(16384)
        WT0 = carve(16384)
        WT1 = carve(16384)
        aT = V(Z1, BF16, 16, 1152)
        mixT = V(Z1, BF16, 16, 1026)
        hnT = V(Z1, BF16, 16, 1026)
        qT = V(Z2, BF16, 8, 1026)
        kT = V(Z2 + 16416, BF16, 8, 2048)
        Vext = V(Z2 + 49184, BF16, 16, 8, 130)
        resid = V(Z2, F32, 8, 2048)
        hres = V(Z2 + 65536, F32, 2048)
        sg = V(Z2 + 73728, BF16, 4, 1024)
        gu = V(Z3, BF16, 9, 1024)
        vn = V(Z3 + 18432, BF16, 9, 1024)
        XT = [V(Z3 + i * 8192, F32, 2048) for i in range(2)]
        XB = [V(Z3 + 16384 + i * 4096, BF16, 2048) for i in range(2)]
        grep = V(Z3 + 24576, F32, 2048)
        Pbuf = V(Z3, BF16, 2, 16, 384)
        actT = V(Z3, BF16, 8, 1024)
        hfull = [V(Z3 + 16384 + i * 4104, F32, 1026) for i in range(2)]
        accb = [V(Z3 + 16384 + 8208 + i * 4096, F32, 1024) for i in range(2)]
        OT = [V(Z3 + i * 8192, F32, 2048) for i in range(2)]
        lnrep = V(Z4, F32, 2, 1024)
        tmpf = [V(Z4 + 8192 + i * 2048, F32, 512) for i in range(4)]
        tmpN = V(Z4 + 8192, F32, 1024)
        qtok = [V(Z4 + i * 1024, BF16, 512) for i in range(2)]
        ropet = [V(Z4 + 2048 + i * 256, F32, 8, 8) for i in range(4)]
        qfs = [V(Z4 + 4096 + i * 2048, F32, 512) for i in range(2)]
        gmT = V(Z4, BF16, 1024)
        otok = V(Z4, BF16, 3, 128)
        o0 = [V(Z4 + 1024 + i * 512, F32, 128) for i in range(2)]
        o1 = [V(Z4 + 2048 + i * 512, F32, 128) for i in range(2)]
        ojunk = V(Z4 + 3072, F32, 128)
        wpu = V(Z4, BF16, 2, 2048)
        pT = V(Z4 + 8192, BF16, 2, 1024)
        ptile = V(Z4 + 12288, F32, 256)
        pbt = V(Z4 + 13312, BF16, 256)
        gfin = V(Z4, F32, 2048)
        WTR = V(WT0, BF16, 16384)
        assert WT1 == WT0 + 16384
        identf = V(carve(512), F32, 128)
        identb = V(carve(256), BF16, 128)
        oneb = V(carve(64), BF16, 16)
        gcols = V(carve(192), F32, 3, 16)
        gsub = V(carve(512), F32, 128)
        gsub8 = V(carve(512), F32, 128)
        lamv = V(carve(1024), F32, 4, 64)
        lamt = V(carve(512), F32, 2, 64)
        lams = V(carve(32), F32, 8)
        neglam = lams[:, 4:5]
        mhalf = V(carve(32), F32, 8)
        wsTb = V(carve(2048), BF16, 8, 128)
        bsT = V(carve(32), F32, 8)
        cw = V(carve(1408), F32, 88, 4)
        posi = V(carve(64), I32, 16)
        posf = V(carve(64), F32, 16)
        invf = V(carve(32), F32, 8)
        ang = V(carve(512), F32, 16, 8)
        angk = V(carve(512), F32, 16, 8)
        angi = V(carve(512), I32, 16, 8)
        cosT = V(carve(512), F32, 16, 8)
        sinT = V(carve(512), F32, 16, 8)
        halot = V(carve(2048), BF16, 8, 128)
        stat = V(carve(1024), F32, 256)
        bnst = V(carve(9 * 2 * 6 * 4), F32, 9, 2, 6)
        bnmv = V(carve(9 * 2 * 4), F32, 9, 2)

        Rb = [Reg("bank%d" % i) for i in range(8)]
        bankbf = [b[:, :].bitcast(BF16) for b in banks]
        DS_small = P.new_dsem()
        DS_w = [P.new_dsem() for _ in range(4)]
        DS_x = [P.new_dsem(), P.new_dsem()]
        DS_o = [P.new_dsem(), P.new_dsem()]
        DS_misc = P.new_dsem()
        DS_grep = P.new_dsem()
        DS_r = [P.new_dsem() for _ in range(9)]
        DS_wpu = P.new_dsem()
        DS_pt = P.new_dsem()
        R_w = [Reg("wq%d" % i) for i in range(4)]
        R_const = Reg("const")

        P.op("pool", lambda e: e.memset(identf, 0.0), writes=[R_const])
        P.op("pool", lambda e: e.affine_select(out=identf, in_=identf, pattern=[[-1, 128]], compare_op=ALU.not_equal,
                                               fill=1.0, base=0, channel_multiplier=1), writes=[R_const])
        P.op("dve", lambda e: e.tensor_copy(out=identb, in_=identf), reads=[R_const], writes=[R_const])
        P.op("dve", lambda e: e.memset(oneb, 1.0), writes=[R_const])
        P.op("dve", lambda e: e.memset(mhalf, -0.5), writes=[R_const])
        for dst, src in ((gcols, gcols_d), (gsub, gsub_d), (lamv, lamv_d), (bsT, bsT_d), (cw, cw_d), (posi, pos_d),
                         (invf, invf_d)):
            P.dma("sp", (lambda d_, s_: lambda e: e.dma_start(out=d_, in_=s_))(dst, src), DS_small, writes=[R_const])
        P.dma("pool", lambda e: e.dma_start(out=wsTb, in_=wsT_d), DS_wpu, writes=[R_const])
        P.barrier()
        for i in range(2):
            P.op("dve", (lambda i: lambda e: e.tensor_tensor(out=lamt[:, i, :], in0=lamv[:, 2 * i, :], in1=lamv[:, 2 * i + 1, :],
                                                             op=ALU.mult))(i), reads=[R_const], writes=[R_const])
            P.op("dve", (lambda i: lambda e: e.tensor_reduce(out=lams[:, i:i + 1], in_=lamt[:, i, :], axis=AX.X, op=ALU.add))(i),
                 reads=[R_const], writes=[R_const])
        P.op("act", lambda e: e.activation(out=lams[:, 2:4], in_=lams[:, 0:2], func=AF.Exp), reads=[R_const], writes=[R_const])
        P.op("dve", lambda e: e.tensor_tensor(out=lams[:, 4:5], in0=lams[:, 3:4], in1=lams[:, 2:3], op=ALU.subtract),
             reads=[R_const], writes=[R_const])
        P.op("dve", lambda e: e.tensor_scalar(out=lams[:, 4:5], in0=lams[:, 4:5], scalar1=-0.2, scalar2=None, op0=ALU.add),
             reads=[R_const], writes=[R_const])
        P.op("dve", lambda e: e.tensor_scalar(out=gsub8, in0=gsub, scalar1=0.8, scalar2=None, op0=ALU.mult),
             reads=[R_const], writes=[R_const])
        P.op("dve", lambda e: e.tensor_copy(out=posf, in_=posi), reads=[R_const], writes=[R_const])
        P.op("dve", lambda e: e.tensor_tensor(out=ang, in0=posf.unsqueeze(2).broadcast_to([128, 16, 8]),
                                              in1=invf.unsqueeze(1).broadcast_to([128, 16, 8]), op=ALU.mult),
             reads=[R_const], writes=[R_const])
        for dstT, shift in ((sinT, 0.0), (cosT, math.pi / 2)):
            P.op("dve", (lambda sh: lambda e: e.tensor_scalar(out=angk, in0=ang, scalar1=sh, scalar2=1.0 / TWO_PI, op0=ALU.add,
                                                              op1=ALU.mult))(shift), reads=[R_const], writes=[R_const])
            P.op("dve", lambda e: e.tensor_copy(out=angi, in_=angk), reads=[R_const], writes=[R_const])
            P.op("dve", lambda e: e.tensor_copy(out=angk, in_=angi), reads=[R_const], writes=[R_const])
            P.op("dve", lambda e: e.tensor_scalar(out=angk, in0=angk, scalar1=-TWO_PI, scalar2=None, op0=ALU.mult),
                 reads=[R_const], writes=[R_const])
            P.op("dve", (lambda sh: lambda e: e.scalar_tensor_tensor(out=angk, in0=ang, scalar=sh, in1=angk, op0=ALU.add,
                                                                     op1=ALU.add))(shift), reads=[R_const], writes=[R_const])
            P.op("dve", (lambda d_: lambda e: e.tensor_scalar(out=d_, in0=angk, scalar1=math.pi, scalar2=-TWO_PI, op0=ALU.is_gt,
                                                              op1=ALU.mult))(dstT), reads=[R_const], writes=[R_const])
            P.op("dve", (lambda d_: lambda e: e.tensor_tensor(out=angk, in0=angk, in1=d_, op=ALU.add))(dstT),
                 reads=[R_const], writes=[R_const])
            P.op("dve", (lambda d_: lambda e: e.tensor_scalar(out=d_, in0=angk, scalar1=-math.pi, scalar2=TWO_PI, op0=ALU.is_lt,
                                                              op1=ALU.mult))(dstT), reads=[R_const], writes=[R_const])
            P.op("dve", (lambda d_: lambda e: e.tensor_tensor(out=angk, in0=angk, in1=d_, op=ALU.add))(dstT),
                 reads=[R_const], writes=[R_const])
            P.op("dve", lambda e: e.tensor_scalar(out=angk, in0=angk, scalar1=-3.14159, scalar2=3.14159, op0=ALU.max, op1=ALU.min),
                 reads=[R_const], writes=[R_const])
            P.op("act", (lambda d_: lambda e: e.activation(out=d_, in_=angk, func=AF.Sin))(dstT), reads=[R_const], writes=[R_const])
        P.barrier()

        st_i = [0]

        def stcol(n=1):
            i = st_i[0]
            if i + n > 256:
                i = 0
            st_i[0] = i + n
            return stat[:, i:i + n]

        def rsqrt_col(src_col, mult, R_s, parts=128, lnexp=False):
            v = stcol()
            r = stcol()
            P.op("dve", lambda e: e.tensor_scalar(out=v[0:parts], in0=src_col, scalar1=mult, scalar2=EPS, op0=ALU.mult, op1=ALU.add),
                 reads=[R_s], writes=[R_s])
            if lnexp:
                P.op("act", lambda e: e.activation(out=v[0:parts], in_=v[0:parts], func=AF.Ln), reads=[R_s], writes=[R_s])
                P.op("act", lambda e: e.activation(out=r[0:parts], in_=v[0:parts], func=AF.Exp, scale=-0.5), reads=[R_s], writes=[R_s])
            else:
                P.op("act", lambda e: e.activation(out=v[0:parts], in_=v[0:parts], func=AF.Sqrt), reads=[R_s], writes=[R_s])
                P.op("dve", lambda e: e.reciprocal(out=r[0:parts], in_=v[0:parts]), reads=[R_s], writes=[R_s])
            return r[0:parts]

        xs_i = [0]
        bk_i = [0]

        R_grep = Reg("grep")

        def load_grep(gi):
            P.dma("sp", lambda e: e.dma_start(out=grep, in_=grep_d[gi]), DS_grep, writes=[R_grep])

        def norm_A(src, R_src):
            s = xs_i[0] % 2
            xs_i[0] += 1
            R_s = Reg()
            ss = stcol()
            xb = XB[s]
            R_xb = R_XB[s]
            P.op("act", lambda e: e.activation(out=xb, in_=src, func=AF.Square, accum_out=ss), reads=[R_src], writes=[R_xb, R_s])
            r = rsqrt_col(ss, 1.0 / D, R_s)
            P.op("dve", lambda e: e.scalar_tensor_tensor(out=xb, in0=src, scalar=r, in1=grep, op0=ALU.mult, op1=ALU.mult),
                 reads=[R_src, R_s, R_grep], writes=[R_xb])
            return xb, R_xb

        def norm_B(st, dstT, tcol, R_dst, bankset):
            xb, R_xb = st
            for half in range(2):
                b = bankset[bk_i[0] % len(bankset)]
                bk_i[0] += 1
                for c in range(8):
                    kc = half * 8 + c
                    P.op("pe", lambda e: e.transpose(out=bankbf[b][:, c * 128:(c + 1) * 128],
                                                     in_=xb[:, kc * 128:(kc + 1) * 128], identity=identb),
                         reads=[R_xb, R_const], writes=[Rb[b]], inc=(c == 7))
                dst = dstT[:, half * 8:(half + 1) * 8, tcol:tcol + 128]
                src_ps = bankbf[b][:, :].rearrange("p (c t) -> p c t", c=8)
                if half == 0:
                    P.op("act", lambda e: e.activation(out=dst, in_=src_ps, func=AF.Copy), reads=[Rb[b]], writes=[R_dst])
                else:
                    P.op("dve", lambda e: e.tensor_copy(out=dst, in_=src_ps), reads=[Rb[b]], writes=[R_dst])

        def norm_to_T(src, R_src, gi, dstT, tcol, R_dst, bankset):
            norm_B(norm_A(src, R_src), dstT, tcol, R_dst, bankset)

        def norm_pipeline(items, bankset=(0, 1, 2, 3)):
            prev = None
            for (src_fn, dstT, tcol, R_dst, post) in items:
                src, R_src = src_fn()
                st = norm_A(src, R_src)
                if prev is not None:
                    norm_B(prev[0], prev[1], prev[2], prev[3], list(bankset))
                    if prev[4] is not None:
                        prev[4]()
                prev = (st, dstT, tcol, R_dst, post)
            norm_B(prev[0], prev[1], prev[2], prev[3], list(bankset))
            if prev[4] is not None:
                prev[4]()

        R_XT = [Reg(), Reg()]
        R_XB = [Reg(), Reg()]
        wslot = [0]
        lb_i = [0]

        tasks = []

        def add_task(fn, spec=None):
            tasks.append((spec, fn, P.skip))

        def section(fn):
            def f(s_):
                flush_deferred()
                fn()
            add_task(f)
            return fn

        def wview(spec, q0):
            wd, k0, nk, c0, ncols = spec
            return WTR[:, q0 * 4096:q0 * 4096 + nk * ncols].rearrange("p (k n) -> p k n", k=nk)

        def wquarters(spec):
            return (spec[2] * spec[4] * 2 + 8191) // 8192

        def wregs(spec, q0):
            return [R_w[q0 + i] for i in range(wquarters(spec))]

        def issue_weights(spec, q0):
            wd, k0, nk, c0, ncols = spec
            wv = wview(spec, q0)
            step = max(1, 2048 // ncols)
            g = 0
            while g < nk:
                n = min(step, nk - g)
                q = q0 + (g * ncols * 2) // 8192
                P.dma("pool", lambda e: e.dma_start(
                    out=wv[:, g:g + n, :],
                    in_=wd[(k0 + g) * 128:(k0 + g + n) * 128, c0:c0 + ncols].rearrange("(k p) n -> p k n", p=128)),
                      DS_w[q], writes=[R_w[q]])
                g += n

        def linear_tm(wd, k0, nk, c0, tiles, lhs_fn, epi, bankset, ncols=512):
            tiles = list(tiles)

            spec = (wd, k0, nk, c0, ncols)

            def fn(q0):
                wv = wview(spec, q0)
                wr = wregs(spec, q0)
                for t in tiles:
                    b = bankset[lb_i[0] % len(bankset)]
                    lb_i[0] += 1
                    for kc in range(nk):
                        lhs, R_l = lhs_fn(kc, t)
                        M = lhs.shape[-1]
                        P.op("pe", lambda e: e.matmul(banks[b][0:M, 0:ncols], lhsT=lhs, rhs=wv[:, kc, :],
                                                      start=(kc == 0), stop=(kc == nk - 1)),
                             reads=wr + [R_l], writes=[Rb[b]], inc=(kc == nk - 1))
                    flush_deferred()
                    d_ = epi(t, b)
                    if d_ is not None:
                        deferred.append(d_)
            add_task(fn, spec)

        deferred = []

        def flush_deferred():
            while deferred:
                deferred.pop(0)()

        def barrier_task():
            def f(s_):
                flush_deferred()
                P.barrier()
            add_task(f)

        def load_x_tile(row0):
            s = xs_i[0] % 2
            P.dma("sp", lambda e: e.dma_start(out=XT[s], in_=x_d[row0:row0 + 128, :]), DS_x[s], writes=[R_XT[s]])
            return XT[s], R_XT[s]

        R_aT = [Reg("aT%d" % i) for i in range(9)]
        R_qT = Reg("qT")
        R_kT = Reg("kT")
        R_V = Reg("V")
        R_gu = [Reg() for _ in range(9)]
        R_vn = [Reg() for _ in range(9)]
        R_mixA = Reg("mixA")
        R_mixG = Reg("mixG")
        R_tmp = [Reg() for _ in range(4)]
        R_qtok = [Reg(), Reg()]
        R_rope = Reg()
        R_lnrep = Reg()
        R_qf = [Reg(), Reg()]
        R_halot = Reg()
        R_resid = [Reg("resid%d" % i) for i in range(9)]
        R_hnT = [Reg() for _ in range(9)]
        tmp_i = [0]

        def rope_epi(dstT, R_dst, hh, colfn):
            def epi(t, b):
                s = tmp_i[0] % 2
                tmp_i[0] += 1
                qt = qtok[s]
                bank3 = banks[b][:, :].rearrange("p (g d) -> p g d", d=64)
                qt3 = qt.rearrange("p (g d) -> p g d", d=64)
                cosb = cosT[:, t, :].unsqueeze(1).broadcast_to([128, 8, 8])
                sinb = sinT[:, t, :].unsqueeze(1).broadcast_to([128, 8, 8])
                qf = qfs[s]
                qf3 = qf.rearrange("p (g d) -> p g d", d=64)
                P.op("act", lambda e: e.activation(out=qf, in_=banks[b][:, :], func=AF.Copy), reads=[Rb[b]], writes=[R_qf[s]])
                P.op("act", lambda e: e.activation(out=qt, in_=banks[b][:, :], func=AF.Copy), reads=[Rb[b]], writes=[R_qtok[s]])
                P.op("dve", lambda e: e.tensor_tensor(out=ropet[0], in0=qf3[:, :, 0:8], in1=cosb, op=ALU.mult),
                     reads=[R_qf[s], R_const], writes=[R_rope])
                P.op("dve", lambda e: e.tensor_tensor(out=ropet[1], in0=qf3[:, :, 8:16], in1=sinb, op=ALU.mult),
                     reads=[R_qf[s], R_const], writes=[R_rope])
                P.op("dve", lambda e: e.tensor_tensor(out=qt3[:, :, 0:8], in0=ropet[0], in1=ropet[1], op=ALU.subtract),
                     reads=[R_rope], writes=[R_qtok[s]])
                P.op("dve", lambda e: e.tensor_tensor(out=ropet[2], in0=qf3[:, :, 8:16], in1=cosb, op=ALU.mult),
                     reads=[R_qf[s], R_const], writes=[R_rope])
                P.op("dve", lambda e: e.tensor_tensor(out=ropet[3], in0=qf3[:, :, 0:8], in1=sinb, op=ALU.mult),
                     reads=[R_qf[s], R_const], writes=[R_rope])
                P.op("dve", lambda e: e.tensor_tensor(out=qt3[:, :, 8:16], in0=ropet[2], in1=ropet[3], op=ALU.add),
                     reads=[R_rope], writes=[R_qtok[s]])
                tb = 4 + (tmp_i[0] % 2)

                def part2():
                    rope_part2(qt, s, tb, t)
                return part2

            def rope_part2(qt, s, tb, t):
                for j in range(4):
                    P.op("pe", (lambda j: lambda e: e.transpose(out=bankbf[tb][:, j * 128:(j + 1) * 128],
                                                                in_=qt[:, j * 128:(j + 1) * 128], identity=identb))(j),
                         reads=[R_qtok[s], R_const], writes=[Rb[tb]], inc=(j == 3))
                col, n = colfn(t)
                if n == 128:
                    P.op("dve", lambda e: e.tensor_copy(out=dstT[:, 4 * hh:4 * hh + 4, col:col + n],
                                                        in_=bankbf[tb][:, 0:512].rearrange("p (j c) -> p j c", j=4)),
                         reads=[Rb[tb]], writes=[R_dst])
                else:
                    P.op("dve", lambda e: e.tensor_copy(out=halot[:, 0:4, :], in_=bankbf[tb][:, 0:512].rearrange("p (j c) -> p j c", j=4)),
                         reads=[Rb[tb]], writes=[R_halot])
                    P.op("dve", lambda e: e.tensor_copy(out=dstT[:, 4 * hh:4 * hh + 4, col:col + n], in_=halot[:, 0:4, 0:n]),
                         reads=[R_halot], writes=[R_dst])
            return epi

        def v_epi(hh):
            def epi(t, b):
                P.op("act", lambda e: e.activation(out=Vext[:, t, 4 * hh:4 * hh + 4, 0:128],
                                                   in_=banks[b][:, :].rearrange("p (j d) -> p j d", j=4), func=AF.Copy),
                     reads=[Rb[b]], writes=[R_V])
            return epi

        own_col = lambda t: (t * 128, 128) if t < 8 else (1024, 1)
        lin_banks = [0, 1, 2, 3]

        P.skip = upto < 1
        R_aTp = R_aT

        @section
        def sec_p1a():
            P.op("pool", lambda e: e.memset(Vext[:, :, :, 128:130], 1.0), writes=[R_V])
            load_grep(0)
            norm_pipeline([((lambda t=8 + i: load_x_tile(t * 128)), aT, i * 128, R_aTp[i], None) for i in range(8)])
        for hh in range(2):
            linear_tm(w_in_d, 0, 16, 1024 + hh * 512, range(8), lambda kc, i: (aT[:, kc, i * 128:(i + 1) * 128], R_aTp[i]),
                      (lambda ep: lambda i, b: ep(8 + i, b))(rope_epi(kT, R_kT, hh, lambda t: (t * 128, 128))), lin_banks)
        for hh in range(2):
            linear_tm(w_in_d, 0, 16, 2048 + hh * 512, range(8), lambda kc, i: (aT[:, kc, i * 128:(i + 1) * 128], R_aTp[i]),
                      (lambda ep: lambda i, b: ep(8 + i, b))(v_epi(hh)), lin_banks)

        P.skip = upto < 2
        @section
        def sec_p1b():
            norm_pipeline([((lambda t=t: load_x_tile(t * 128)), aT, t * 128, R_aT[t], None) for t in range(9)])
        barrier_task()

        @section
        def sec_ln_load():
            P.dma("sp", lambda e: e.dma_start(out=lnrep, in_=lnrep_d), DS_misc, writes=[R_lnrep])
        own_lhs = lambda kc, t: (aT[:, kc, t * 128:(t + 1) * 128], R_aT[t])

        def gu_epi(hcb):
            def epi(t, b):
                P.op("act", lambda e: e.activation(out=gu[:, t, hcb * 512:(hcb + 1) * 512], in_=banks[b][:, :], func=AF.Gelu),
                     reads=[Rb[b]], writes=[R_gu[t]])
            return epi

        def gv_epi(hcb):
            def epi(t, b):
                s = 2 + (tmp_i[0] % 2)
                tmp_i[0] += 1
                tf = tmpf[s]
                P.op("act", lambda e: e.activation(out=tf, in_=banks[b][:, :], func=AF.Gelu), reads=[Rb[b]], writes=[R_tmp[s]])
                P.op("dve", lambda e: e.bn_stats(out=bnst[:, t, hcb, :], in_=tf), reads=[R_tmp[s]], writes=[R_vn[t]])
                P.op("act", lambda e: e.activation(out=vn[:, t, hcb * 512:(hcb + 1) * 512], in_=tf, func=AF.Copy), reads=[R_tmp[s]],
                     writes=[R_vn[t]])
                if hcb == 1:
                    P.op("dve", lambda e: e.bn_aggr(out=bnmv[:, t, :], in_=bnst[:, t, :, :].rearrange("p a b -> p (a b)")),
                         reads=[R_vn[t]], writes=[R_vn[t]])
                    r = rsqrt_col(bnmv[:, t, 1:2], 1.0, R_vn[t])
                    P.op("dve", lambda e: e.scalar_tensor_tensor(out=tmpN, in0=vn[:, t, :], scalar=bnmv[:, t, 0:1], in1=lnrep[:, 0, :],
                                                                 op0=ALU.subtract, op1=ALU.mult),
                         reads=[R_vn[t], R_lnrep], writes=[R_tmp[0], R_tmp[1]])
                    P.op("dve", lambda e: e.scalar_tensor_tensor(out=vn[:, t, :], in0=tmpN, scalar=r, in1=lnrep[:, 1, :],
                                                                 op0=ALU.mult, op1=ALU.add),
                         reads=[R_lnrep, R_tmp[0], R_tmp[1]], writes=[R_vn[t]])
            return epi

        for hcb in range(2):
            linear_tm(w_in_d, 0, 16, 3072 + hcb * 512, range(9), own_lhs, gu_epi(hcb), lin_banks)
        for hcb in range(2):
            linear_tm(w_in_d, 0, 16, 4096 + hcb * 512, range(9), own_lhs, gv_epi(hcb), lin_banks)

        P.skip = upto < 3
        sp_i = [0]

        @section
        def sec_p3():
            for t in range(9):
                for j in range(2):
                    b = 4 + (sp_i[0] % 4)
                    sp_i[0] += 1
                    for hq in range(4):
                        h = 4 * j + hq
                        P.op("pe", lambda e: e.matmul(banks[b][:, hq * 128:(hq + 1) * 128], lhsT=wsTb[:, h, :],
                                                      rhs=vn[:, t, h * 128:(h + 1) * 128], start=True, stop=True),
                             reads=[R_vn[t], R_const], writes=[Rb[b]], inc=(hq == 3))
                    s = tmp_i[0] % 2
                    tmp_i[0] += 1
                    tf = tmpf[s]
                    P.op("dve", lambda e: e.tensor_tensor(
                        out=tf.rearrange("p (j d) -> p j d", j=4), in0=banks[b][:, :].rearrange("p (j d) -> p j d", j=4),
                        in1=bsT[:, 4 * j:4 * j + 4].unsqueeze(2).broadcast_to([128, 4, 128]), op=ALU.add),
                         reads=[Rb[b], R_const], writes=[R_tmp[s]])
                    P.op("pool", lambda e: e.tensor_tensor(out=gu[:, t, j * 512:(j + 1) * 512], in0=tf,
                                                           in1=gu[:, t, j * 512:(j + 1) * 512], op=ALU.mult),
                         reads=[R_tmp[s]], writes=[R_gu[t]])
        barrier_task()

        @section
        def sec_qpad():
            P.op("pool", lambda e: e.memset(qT[:, :, 1025:1026], 0.0), writes=[R_qT])
        for hh in range(2):
            linear_tm(w_in_d, 0, 16, hh * 512, range(9), own_lhs, rope_epi(qT, R_qT, hh, own_col), lin_banks)
        for hh in range(2):
            linear_tm(w_in_d, 0, 16, 1024 + hh * 512, range(8), own_lhs, rope_epi(kT, R_kT, hh, lambda t: (t * 128, 128)), lin_banks)
        for hh in range(2):
            linear_tm(w_in_d, 0, 16, 2048 + hh * 512, range(8), own_lhs, v_epi(hh), lin_banks)
        barrier_task()

        @section
        def sec_gmT():
            P.op("pool", lambda e: e.memset(mixT[:, :, 1025:1026], 0.0), writes=[R_mixG, R_mixA])
            for t in range(9):
                tb = 4 + (t % 2)
                for c in range(8):
                    P.op("pe", lambda e: e.transpose(out=bankbf[tb][:, c * 128:(c + 1) * 128],
                                                     in_=gu[:, t, c * 128:(c + 1) * 128], identity=identb),
                         reads=[R_gu[t], R_const], writes=[Rb[tb]], inc=(c == 7))
                col, n = own_col(t)
                if n == 128:
                    P.op("dve", lambda e: e.tensor_copy(out=mixT[:, 8:16, col:col + n], in_=bankbf[tb][:, :].rearrange("p (c t) -> p c t", c=8)),
                         reads=[Rb[tb]], writes=[R_mixG])
                else:
                    P.op("dve", lambda e: e.tensor_copy(out=halot, in_=bankbf[tb][:, :].rearrange("p (c t) -> p c t", c=8)),
                         reads=[Rb[tb]], writes=[R_halot])
                    P.op("dve", lambda e: e.tensor_copy(out=mixT[:, 8:16, col:col + n], in_=halot[:, :, 0:n]), reads=[R_halot], writes=[R_mixG])
        barrier_task()

        @section
        def sec_dbg1():
            for nm, ap in (("qT", qT), ("kT", kT), ("gu", gu), ("vn", vn)):
                if nm in dbg:
                    P.dma("sp", lambda e: e.dma_start(out=dbg_d[nm], in_=ap.rearrange("p a b -> p (a b)")), DS_misc)
            if "Vext" in dbg:
                P.dma("sp", lambda e: e.dma_start(out=dbg_d["Vext"], in_=Vext.rearrange("p a b c -> p (a b c)")), DS_misc)
            if "cs" in dbg:
                P.dma("sp", lambda e: e.dma_start(out=dbg_d["cs"][:, 0:128], in_=cosT.rearrange("p a b -> p (a b)")), DS_misc)
                P.dma("sp", lambda e: e.dma_start(out=dbg_d["cs"][:, 128:256], in_=sinT.rearrange("p a b -> p (a b)")), DS_misc)
            if dbg:
                P.barrier()

        P.skip = upto < 4
        Pbuf2 = [V(Z3 + i * 16512, BF16, 2, 16, 258) for i in range(2)]
        qm1 = V(Z3 + 33024, BF16, 8, 1026)
        otok4 = V(Z4 + 12608, BF16, 3, 128)
        o0b = [V(Z4 + 13376 + i * 512, F32, 128) for i in range(2)]
        o1b = [V(Z4 + 14400 + i * 512, F32, 128) for i in range(2)]
        ojunk4 = V(Z4 + 15424, F32, 128)

        @section
        def sec_p4():
            R_P = [[[Reg() for _ in range(16)] for _ in range(2)] for _ in range(2)]
            R_otok = Reg()
            R_o = [Reg(), Reg()]
            qblocks = [(0, 256), (256, 512), (512, 768), (768, 1025)]
            blocks = [(h, c0, c1) for h in range(8) for (c0, c1) in qblocks]
            s_i = [0]
            o_i = [0]
            R_qm = Reg()
            R_qm2 = Reg()
            P.op("dve", lambda e: e.memset(qm1[0:64], 0.0), writes=[R_qm])
            P.op("act", lambda e: e.activation(out=qm1[64:128, 0:4, :], in_=qT[64:128, 0:4, :], func=AF.Copy), reads=[R_qT], writes=[R_qm2])
            P.op("dve", lambda e: e.tensor_copy(out=qm1[64:128, 4:8, :], in_=qT[64:128, 4:8, :]), reads=[R_qT], writes=[R_qm])
            P.op("dve", lambda e: e.memset(qT[64:128], 0.0), reads=[R_qm2], writes=[R_qT])
            P.op("dve", lambda e: e.tensor_copy(out=stcol(), in_=mhalf[:, 0:1]), reads=[R_qm2], writes=[R_qm])
            qsrc = [qT, qm1]
            R_qs = [R_qT, R_qm]

            def emit_qk(bi, kc):
                h, c0, c1 = blocks[bi]
                n = c1 - c0
                pb = Pbuf2[bi % 2]
                sb = (s_i[0] % 2) * 2
                s_i[0] += 1
                for c in range(2):
                    P.op("pe", lambda e: e.matmul(banks[sb + c][:, 0:n], lhsT=kT[:, h, kc * 128:(kc + 1) * 128],
                                                  rhs=qsrc[c][:, h, c0:c1], start=True, stop=True),
                         reads=[R_kT, R_qs[c]], writes=[Rb[sb + c]])
                for c in range(2):
                    P.op("act", lambda e: e.activation(out=pb[:, c, kc, 0:n], in_=banks[sb + c][:, 0:n], func=AF.Exp, scale=0.125),
                         reads=[Rb[sb + c]], writes=[R_P[bi % 2][c][kc]])

            def av_items(bi):
                h, c0, c1 = blocks[bi]
                n = c1 - c0
                pb = Pbuf2[bi % 2]
                RP = R_P[bi % 2]
                ntile = (n + 127) // 128
                tb = 6 + (bi % 2)
                items = []
                e2 = []
                for j in range(ntile):
                    m = 128
                    jc0 = j * 128 if (j + 1) * 128 <= n else n - 128
                    ob = 4 + (o_i[0] % 2)
                    o_i[0] += 1
                    s = o_i[0] % 2
                    R_s = R_o[s]
                    O = banks[ob]
                    st = {}

                    def mk_av(c, j=j, m=m, ob=ob, jc0=jc0):
                        def f():
                            for kc in range(16):
                                P.op("pe", lambda e: e.matmul(banks[ob][0:m, c * 130:c * 130 + 129], lhsT=pb[:, c, kc, jc0:jc0 + m],
                                                              rhs=Vext[:, kc, h, 0:129], start=(kc == 0), stop=(kc == 15)),
                                     reads=[RP[c][kc], R_V], writes=[Rb[ob]], inc=(kc == 15 and c == 1))
                        return f

                    def e1(j=j, m=m, ob=ob, s=s, R_s=R_s, O=O, st=st):
                        rr = stcol(2)
                        P.op("dve", lambda e: e.reciprocal(out=rr[0:m, 0:1], in_=O[0:m, 128:129]), reads=[Rb[ob]], writes=[R_s])
                        P.op("dve", lambda e: e.reciprocal(out=rr[0:m, 1:2], in_=O[0:m, 258:259]), reads=[Rb[ob]], writes=[R_s])
                        P.op("dve", lambda e: e.tensor_tensor(out=rr[0:m, 1:2], in0=rr[0:m, 1:2], in1=neglam[0:m], op=ALU.mult),
                             reads=[R_const], writes=[R_s])
                        P.op("dve", lambda e: e.tensor_scalar(out=o0b[s][0:m], in0=O[0:m, 0:128], scalar1=rr[0:m, 0:1], scalar2=None,
                                                              op0=ALU.mult), reads=[Rb[ob]], writes=[R_s])
                        P.op("dve", lambda e: e.scalar_tensor_tensor(out=o1b[s][0:m], in0=O[0:m, 130:258], scalar=rr[0:m, 1:2],
                                                                     in1=o0b[s][0:m], op0=ALU.mult, op1=ALU.add),
                             reads=[Rb[ob]], writes=[R_s])
                        ssq = stcol()
                        P.op("dve", lambda e: e.scalar_tensor_tensor(out=ojunk4[0:m], in0=o1b[s][0:m], scalar=1.0, in1=o1b[s][0:m],
                                                                     op0=ALU.mult, op1=ALU.mult, accum_out=ssq[0:m]),
                             reads=[], writes=[R_s])
                        v = stcol()
                        P.op("dve", lambda e: e.tensor_scalar(out=v[0:m], in0=ssq[0:m], scalar1=1.0 / 128, scalar2=EPS, op0=ALU.mult,
                                                              op1=ALU.add), reads=[], writes=[R_s])
                        st["v"] = v

                    def e2f(m=m, R_s=R_s, st=st):
                        v = st["v"]
                        r = stcol()
                        P.op("act", lambda e: e.activation(out=v[0:m], in_=v[0:m], func=AF.Ln), reads=[R_s], writes=[R_s])
                        P.op("act", lambda e: e.activation(out=r[0:m], in_=v[0:m], func=AF.Exp, scale=-0.5), reads=[R_s], writes=[R_s])
                        st["r"] = r

                    def e3(j=j, m=m, s=s, R_s=R_s, st=st, last=(j == ntile - 1)):
                        r = st["r"]
                        P.op("dve", lambda e: e.scalar_tensor_tensor(out=otok4[0:m, j, :], in0=o1b[s][0:m], scalar=r[0:m], in1=gsub8[0:m],
                                                                     op0=ALU.mult, op1=ALU.mult), reads=[R_const, R_s], writes=[R_otok])
                        P.op("pe", lambda e: e.transpose(out=bankbf[tb][:, j * 128:j * 128 + m], in_=otok4[0:m, j, :],
                                                         identity=identb[0:m, 0:m]),
                             reads=[R_otok, R_const], writes=[Rb[tb]])
                        if last:
                            nm = (n // 128) * 128
                            P.op("dve", lambda e: e.tensor_copy(out=mixT[:, h, c0:c0 + nm], in_=bankbf[tb][:, 0:nm]),
                                 reads=[Rb[tb]], writes=[R_mixA])
                            if nm < n:
                                P.op("dve", lambda e: e.tensor_copy(out=mixT[:, h, c0 + nm:c0 + n], in_=bankbf[tb][:, nm + 127:nm + 128]),
                                     reads=[Rb[tb]], writes=[R_mixA])
                    items += [mk_av(0), mk_av(1), e1]
                    e2.append((e2f, e3))
                out = []
                k = 0
                for j in range(ntile):
                    out += items[3 * j:3 * j + 3]
                    if j >= 1:
                        out += [e2[j - 1][0], e2[j - 1][1]]
                out += [e2[ntile - 1][0]]
                out += [e2[ntile - 1][1]]
                return out

            nb = len(blocks)
            for bi in range(nb + 1):
                prev = av_items(bi - 1) if bi >= 1 else []
                npv = len(prev)
                done = 0
                for kc in range(16):
                    if bi < nb:
                        emit_qk(bi, kc)
                    want = (npv * (kc + 1) + 15) // 16
                    while done < want:
                        prev[done]()
                        done += 1
        barrier_task()

        @section
        def sec_dbgmix():
            if "mix" in dbg:
                P.dma("sp", lambda e: e.dma_start(out=dbg_d["mix"], in_=mixT.rearrange("p a b -> p (a b)")), DS_misc)
                P.barrier()

        P.skip = upto < 5
        @section
        def sec_xpre():
            for t in range(8):
                P.dma("sp", lambda e: e.dma_start(out=resid[:, t, :], in_=x_d[t * 128:(t + 1) * 128, :]), DS_r[t],
                      writes=[R_resid[t]])
            P.dma("sp", lambda e: e.dma_start(out=hres, in_=x_d[897:1025, :]), DS_r[8], writes=[R_resid[8]])

        def mix_lhs(kc, t):
            col = t * 128 if t < 8 else 897
            return mixT[:, kc, col:col + 128], (R_mixA if kc < 8 else R_mixG)

        def resid_add_epi(cb):
            def epi(t, b):
                if t < 8:
                    dst = resid[:, t, cb * 512:(cb + 1) * 512]
                    P.op("dve", lambda e: e.tensor_tensor(out=dst, in0=banks[b][:, :], in1=dst, op=ALU.add),
                         reads=[Rb[b]], writes=[R_resid[t]])
                else:
                    dst = hres[:, cb * 512:(cb + 1) * 512]
                    P.op("dve", lambda e: e.tensor_tensor(out=dst, in0=banks[b][:, :], in1=dst, op=ALU.add),
                         reads=[Rb[b]], writes=[R_resid[8]])
            return epi

        for cb in range(4):
            linear_tm(w_out_d, 0, 16, cb * 512, range(9), mix_lhs, resid_add_epi(cb), lin_banks)
        barrier_task()

        @section
        def sec_dbgres():
            if "resid" in dbg:
                P.dma("sp", lambda e: e.dma_start(out=dbg_d["resid"][:, 0:8 * 2048], in_=resid.rearrange("p a b -> p (a b)")), DS_misc)
                P.dma("sp", lambda e: e.dma_start(out=dbg_d["resid"][0:1, 8 * 2048:9 * 2048], in_=hres[96:97, :]), DS_misc)
                P.barrier()

        halo16 = V(Z4 + 8192, BF16, 16, 128)
        R_halo16 = Reg()

        def halo_norm(gi):
            norm_to_T(hres, R_resid[8], gi, halo16, 0, R_halo16, [0, 1, 2, 3])
            P.op("dve", lambda e: e.tensor_copy(out=hnT[:, :, 1024], in_=halo16[:, :, 127]), reads=[R_halo16], writes=[R_hnT[8]])

        P.skip = upto < 6

        @section
        def sec_p6():
            load_grep(1)
            norm_pipeline([((lambda t=t: (resid[:, t, :], R_resid[t])), hnT, t * 128, R_hnT[t], None) for t in range(8)])
            halo_norm(1)
        barrier_task()

        @section
        def sec_dbghn():
            if "hnT" in dbg:
                P.dma("sp", lambda e: e.dma_start(out=dbg_d["hnT"], in_=hnT.rearrange("p a b -> p (a b)")), DS_misc)
                P.barrier()

        P.skip = upto < 7
        R_hf = [Reg(), Reg()]
        R_acc = [Reg(), Reg()]
        R_sg = [Reg() for _ in range(4)]
        R_act = Reg()
        tblocks = [(0, 342), (342, 684), (684, 1025)]
        groups = [(0, 1, 2, 3), (4, 5, 6, 7), (8, 9, 10, 11), (12, 13, 14, 15), (16, 17, 18, 19), (20, 21)]
        f_i = [0]
        all_hn = list(R_hnT)

        @section
        def sec_p7init():
            P.op("pool", lambda e: e.memset(hfull[0][:, 0:1], 0.0), writes=[R_hf[0]])
            P.op("pool", lambda e: e.memset(hfull[1][:, 0:1], 0.0), writes=[R_hf[1]])

        def ffn_up_task(j, jl, sel, spec):
            def fn(q0):
                wv = wview(spec, q0)
                wr = wregs(spec, q0)
                for sub in range(2):
                    blk = sel * 44 + j * 2 + sub
                    hs = f_i[0] % 2
                    f_i[0] += 1
                    bset = [0, 1, 2] if hs == 0 else [3, 4, 5]
                    for ti, (c0, c1) in enumerate(tblocks):
                        b = bset[ti]
                        n = c1 - c0
                        for kc in range(16):
                            P.op("pe", lambda e: e.matmul(banks[b][:, 0:n], lhsT=wv[:, kc, sub * 128:(sub + 1) * 128],
                                                          rhs=hnT[:, kc, c0:c1], start=(kc == 0), stop=(kc == 15)),
                                 reads=wr + (all_hn if kc == 0 else []), writes=[Rb[b]], inc=(kc == 15))
                        P.op("act", lambda e: e.activation(out=hfull[hs][:, 1 + c0:1 + c1], in_=banks[b][:, 0:n], func=AF.Copy),
                             reads=[Rb[b]], writes=[R_hf[hs]])
                    hf = hfull[hs]
                    ac = accb[hs]
                    P.op("pool", lambda e: e.tensor_scalar(out=ac, in0=hf[:, 1:1025], scalar1=cw[:, blk, 1:2], scalar2=cw[:, blk, 3:4],
                                                           op0=ALU.mult, op1=ALU.add), reads=[R_hf[hs], R_const], writes=[R_acc[hs]])
                    P.op("dve", lambda e: e.scalar_tensor_tensor(out=ac, in0=hf[:, 0:1024], scalar=cw[:, blk, 0:1], in1=ac, op0=ALU.mult,
                                                                 op1=ALU.add), reads=[R_hf[hs], R_const], writes=[R_acc[hs]])
                    P.op("dve", lambda e: e.scalar_tensor_tensor(out=ac, in0=hf[:, 2:1026], scalar=cw[:, blk, 2:3], in1=ac, op0=ALU.mult,
                                                                 op1=ALU.add), reads=[R_hf[hs], R_const], writes=[R_acc[hs]])
                    if sel == 0:
                        P.op("act", lambda e: e.activation(out=sg[:, sub, :], in_=ac, func=AF.Silu), reads=[R_acc[hs]], writes=[R_sg[sub]])
                    else:
                        lc = jl * 2 + sub
                        P.op("dve", lambda e: e.tensor_tensor(out=actT[:, lc, :], in0=ac, in1=sg[:, sub, :], op=ALU.mult),
                             reads=[R_acc[hs], R_sg[sub]], writes=[R_act])
            return fn

        for grp in groups:
            for jl, j in enumerate(grp):
                for sel in range(2):
                    spec = (w_up_d, 0, 16, sel * DFF + j * 256, 256)
                    add_task(ffn_up_task(j, jl, sel, spec), spec)
            nk = 2 * len(grp)
            k0 = grp[0] * 2
            for cb in range(4):
                linear_tm(w_down_d, k0, nk, cb * 512, range(8), lambda kc, t: (actT[:, kc, t * 128:(t + 1) * 128], R_act),
                          resid_add_epi(cb), [6, 7])
        barrier_task()

        P.skip = upto < 8
        R_wpu = Reg()
        R_pT = Reg()
        R_pt = Reg()

        @section
        def sec_p8a():
            P.dma("pool", lambda e: e.dma_start(out=wpu, in_=w_pleup_d.rearrange("(k p) n -> p k n", p=128)), DS_wpu, writes=[R_wpu])
            load_grep(2)
            def p_post(t):
                def f():
                    P.dma("sp", lambda e: e.dma_start(out=ptile, in_=pp_d[t * 128:(t + 1) * 128, :]), DS_pt, writes=[R_pt])
                    P.op("dve", lambda e: e.tensor_copy(out=pbt, in_=ptile), reads=[R_pt], writes=[R_pt])
                    for kc in range(2):
                        P.op("pe", lambda e: e.transpose(out=bankbf[4][:, kc * 128:(kc + 1) * 128], in_=pbt[:, kc * 128:(kc + 1) * 128],
                                                         identity=identb), reads=[R_pt, R_const], writes=[Rb[4]], inc=(kc == 1))
                    P.op("dve", lambda e: e.tensor_copy(out=pT[:, :, t * 128:(t + 1) * 128],
                                                        in_=bankbf[4][:, 0:256].rearrange("p (k c) -> p k c", k=2)),
                         reads=[Rb[4]], writes=[R_pT])
                return f
            norm_pipeline([((lambda t=t: (resid[:, t, :], R_resid[t])), hnT, t * 128, R_hnT[t], p_post(t)) for t in range(8)])
        barrier_task()

        def gate_epi(cb):
            def epi(t, b):
                pb = 4 + (tmp_i[0] % 4)
                s = tmp_i[0] % 2
                tmp_i[0] += 1
                for kc in range(2):
                    P.op("pe", lambda e: e.matmul(banks[pb][:, :], lhsT=pT[:, kc, t * 128:(t + 1) * 128],
                                                  rhs=wpu[:, kc, cb * 512:(cb + 1) * 512], start=(kc == 0), stop=(kc == 1)),
                         reads=[R_pT, R_wpu], writes=[Rb[pb]], inc=(kc == 1))
                t1 = tmpf[2 + s]
                P.op("act", lambda e: e.activation(out=t1, in_=banks[b][:, :], func=AF.Sigmoid), reads=[Rb[b]], writes=[R_tmp[2 + s]])
                P.op("dve", lambda e: e.tensor_tensor(out=t1, in0=banks[pb][:, :], in1=t1, op=ALU.mult), reads=[Rb[pb]],
                     writes=[R_tmp[2 + s]])
                dst = resid[:, t, cb * 512:(cb + 1) * 512]
                P.op("pool", lambda e: e.tensor_tensor(out=dst, in0=dst, in1=t1, op=ALU.add), reads=[R_tmp[2 + s]],
                     writes=[R_resid[t]])
            return epi

        for cb in range(4):
            linear_tm(w_gate_d, 0, 16, cb * 512, range(8), lambda kc, t: (hnT[:, kc, t * 128:(t + 1) * 128], R_hnT[t]),
                      gate_epi(cb), lin_banks)
        barrier_task()

        P.skip = upto < 9
        R_OT = [Reg(), Reg()]

        @section
        def sec_p9():
            P.dma("sp", lambda e: e.dma_start(out=gfin, in_=gfin_d), DS_misc, writes=[R_const])
            for t in range(8):
                s = t % 2
                R_s = Reg()
                ss = stcol()
                src = resid[:, t, :]
                P.op("dve", lambda e: e.scalar_tensor_tensor(out=OT[s], in0=src, scalar=1.0, in1=src, op0=ALU.mult, op1=ALU.mult,
                                                             accum_out=ss), reads=[R_resid[t]], writes=[R_OT[s], R_s])
                r = rsqrt_col(ss, 1.0 / D, R_s)
                P.op("dve", lambda e: e.scalar_tensor_tensor(out=OT[s], in0=src, scalar=r, in1=gfin, op0=ALU.mult, op1=ALU.mult),
                     reads=[R_resid[t], R_s, R_const], writes=[R_OT[s]])
                P.dma("sp", lambda e: e.dma_start(out=y_d[t * 128:(t + 1) * 128, :], in_=OT[s]), DS_o[s], reads=[R_OT[s]])
        P.skip = False
        barrier_task()

        wt_idx = [i for i, tk in enumerate(tasks) if tk[0] is not None]
        q_of = {}
        live = []
        rp = 0
        issued = 0

        def try_issue():
            nonlocal_rp = rp_box[0]
            k = wt_idx[issued_box[0]]
            nq = wquarters(tasks[k][0])
            start = nonlocal_rp
            if nq == 2 and start % 2 == 1:
                start += 1
            oldest = live[0][1] if live else start
            if start + nq - oldest > 4:
                return False
            P.skip = tasks[k][2]
            q_of[k] = start % 4
            issue_weights(tasks[k][0], q_of[k])
            live.append((k, start))
            rp_box[0] = start + nq
            issued_box[0] += 1
            return True

        rp_box = [0]
        issued_box = [0]
        for i, (spec, fn, skip) in enumerate(tasks):
            while issued_box[0] < len(wt_idx) and (wt_idx[issued_box[0]] <= i or True):
                if not try_issue():
                    break
            if spec is not None:
                assert i in q_of, i
            P.skip = skip
            fn(q_of.get(i))
            if spec is not None:
                assert live and live[0][0] == i
                live.pop(0)
        P.skip = False

        for n_, e_ in P.E.items():
            assert not e_.pending, n_
        with nc.Block() as block:
            @block.tensor
            def _(eng):
                P.replay("pe", eng)

            @block.scalar
            def _(eng):
                P.replay("act", eng)

            @block.vector
            def _(eng):
                P.replay("dve", eng)

            @block.gpsimd
            def _(eng):
                P.replay("pool", eng)

            @block.sync
            def _(eng):
                P.replay("sp", eng)
    return nc


def make_in_maps(inp):
    x = np.asarray(inp["x"], np.float32)
    p = np.asarray(inp["p"], np.float32)[0]
    pos = np.asarray(inp["positions"], np.int32)
    sq = lambda k: np.ascontiguousarray(np.asarray(inp[k], np.float32)[0])
    w_in, w_out, w_up, w_down = sq("w_in"), sq("w_out"), sq("w_up"), sq("w_down")
    w_gate, w_pleup = sq("w_ple_gate"), sq("w_ple_up")
    rep = lambda v: np.ascontiguousarray(np.broadcast_to(v, (128,) + v.shape))
    cols = lambda v: np.ascontiguousarray(v.reshape(-1, 128).T)
    gcols = np.ascontiguousarray(np.stack([cols(sq("g_mix")), cols(sq("g_ffn")), cols(sq("g_ple"))], axis=1))
    gfin = rep(np.asarray(inp["g_final"], np.float32))
    grep = np.ascontiguousarray(np.stack([rep(sq("g_mix")), rep(sq("g_ffn")), rep(sq("g_ple"))]))
    lnrep = rep(np.stack([sq("gmlp_ln_g"), sq("gmlp_ln_b")]))
    gsub = rep(sq("g_subln"))
    lamv = rep(np.stack([sq("lambda_q1"), sq("lambda_k1"), sq("lambda_q2"), sq("lambda_k2")]))
    invf = rep((500000.0 ** (-np.arange(0, 16, 2, dtype=np.float32) / 16)).astype(np.float32))
    ws = sq("w_spatial")
    bs = sq("b_spatial")
    cwv = sq("conv_w")
    cbv = sq("conv_b")
    maps = []
    for c in range(8):
        b, hf = c // 2, c % 2
        if hf == 0:
            xl, pl, posl, wsl, bsl, taps = x[b], p[b, :NOWN], pos[b], ws, bs, (0, 1, 2)
        else:
            xl, pl, posl = x[b, ::-1], p[b, ::-1][:NOWN], pos[b, ::-1]
            wsl, bsl, taps = ws[:, ::-1, ::-1], bs[:, ::-1], (2, 1, 0)
        wsT = np.ascontiguousarray(np.transpose(wsl, (2, 0, 1)))
        bsT = np.ascontiguousarray(bsl.T)
        cw4 = np.stack([cwv[taps[0]], cwv[taps[1]], cwv[taps[2]], cbv], axis=-1)
        cw4 = np.ascontiguousarray(cw4.reshape(88, 128, 4).transpose(1, 0, 2))
        maps.append({
            "x": np.ascontiguousarray(xl), "pp": np.ascontiguousarray(pl),
            "pos": np.ascontiguousarray(posl.reshape(16, 128).T.astype(np.int32)), "invf": invf,
            "gcols": gcols, "gfin": gfin, "grep": grep, "lnrep": lnrep, "gsub": gsub, "lamv": lamv, "wsT": wsT, "bsT": bsT, "cw": cw4,
            "w_in": w_in, "w_out": w_out, "w_up": w_up, "w_down": w_down, "w_gate": w_gate, "w_pleup": w_pleup,
        })
    return maps


def assemble(results):
    out = np.empty((4, S, D), np.float32)
    for c in range(8):
        b, hf = c // 2, c % 2
        y = np.asarray(results[c]["y"], np.float32)
        if hf == 0:
            out[b, :NOWN] = y
        else:
            out[b, NOWN:] = y[::-1]
    return out


def kernel(**inputs):
    nc = build_nc()
    maps = make_in_maps(inputs)
    res = run_bass_kernel_spmd(nc, maps, core_ids=list(range(8)))
    return assemble(res.results)
```

```python
import math
import types
from contextlib import ExitStack
import numpy as np
import concourse.bass as bass
import concourse.mybir as mybir
from concourse.bass_utils import run_bass_kernel_spmd

F32 = mybir.dt.float32
BF16 = mybir.dt.bfloat16
I32 = mybir.dt.int32
AF = mybir.ActivationFunctionType
ALU = mybir.AluOpType
AX = mybir.AxisListType

D = 2048
S = 2048
NOWN = 1024
DFF = 5632
EPS = 1e-6
EPOCH = 6000
TWO_PI = 2.0 * math.pi


def _freeze(fn):
    if fn is None or fn.__closure__ is None:
        return fn
    cells = []
    for c in fn.__closure__:
        try:
            cells.append(types.CellType(c.cell_contents))
        except ValueError:
            cells.append(c)
    return types.FunctionType(fn.__code__, fn.__globals__, fn.__name__, fn.__defaults__, tuple(cells))


class Reg:
    __slots__ = ("w", "r", "name")

    def __init__(self, name=""):
        self.w = None
        self.r = {}
        self.name = name


class DSem:
    def __init__(self, sem):
        self.sem = sem
        self.cnt = 0


class Eng:
    def __init__(self, name):
        self.name = name
        self.items = []
        self.sem = None
        self.cnt = 0
        self.seen = {}
        self.pending = False
        self.owned = set()


class Plan:
    def __init__(self, sems):
        self.pool = list(sems)
        self.E = {n: Eng(n) for n in ("pe", "act", "dve", "pool", "sp")}
        self.dsems = []
        self.skip = False

    def newsem(self):
        return self.pool.pop()

    def new_dsem(self):
        d = DSem(self.newsem())
        self.dsems.append(d)
        return d

    def _waits(self, e, reads, writes):
        deps = {}

        def add(tok):
            if tok is None:
                return
            sem, val = tok
            k = id(sem)
            if k in deps:
                if deps[k][1] < val:
                    deps[k] = (sem, val)
            else:
                deps[k] = (sem, val)

        for r in reads:
            add(r.w)
        for w in writes:
            add(w.w)
            for tok in w.r.values():
                add(tok)
        waits = []
        for k, (sem, val) in deps.items():
            if e.name == "pe" and k in e.owned:
                continue
            if e.seen.get(k, 0) >= val:
                continue
            e.seen[k] = val
            waits.append((sem, val))
        return waits

    def _mark(self, tok, reads, writes):
        k = id(tok[0])
        for r in reads:
            old = r.r.get(k)
            if old is None or old[1] < tok[1]:
                r.r[k] = tok
        for w in writes:
            w.w = tok
            w.r = {}

    def op(self, en, fn, reads=(), writes=(), inc=True):
        if self.skip:
            return
        e = self.E[en]
        if e.sem is None or (e.cnt >= EPOCH and not e.pending):
            e.sem = self.newsem()
            e.owned.add(id(e.sem))
            e.cnt = 0
        waits = self._waits(e, reads, writes)
        tok = (e.sem, e.cnt + 1)
        e.items.append((waits, _freeze(fn), ("inc", e.sem) if inc else None))
        if inc:
            e.cnt += 1
            e.pending = False
        else:
            e.pending = True
        self._mark(tok, reads, writes)

    def dma(self, en, fn, dsem, reads=(), writes=()):
        if self.skip:
            return
        e = self.E[en]
        waits = self._waits(e, reads, writes)
        dsem.cnt += 16
        tok = (dsem.sem, dsem.cnt)
        e.items.append((waits, _freeze(fn), ("dma", dsem.sem)))
        self._mark(tok, reads, writes)

    def barrier(self, engines=("pe", "act", "dve", "pool", "sp")):
        toks = []
        for n, e in self.E.items():
            assert not e.pending
            if e.sem is not None and e.cnt > 0:
                toks.append((e.sem, e.cnt))
        for d in self.dsems:
            if d.cnt > 0 and not getattr(d, "no_barrier", False):
                toks.append((d.sem, d.cnt))
        for n in engines:
            e = self.E[n]
            waits = []
            for sem, val in toks:
                k = id(sem)
                if n == "pe" and k in e.owned:
                    continue
                if e.seen.get(k, 0) >= val:
                    continue
                e.seen[k] = val
                waits.append((sem, val))
            if waits:
                e.items.append((waits, None, None))

    def replay(self, en, eng):
        for waits, fn, post in self.E[en].items:
            for sem, val in waits:
                eng.wait_ge(sem, val)
            if fn is None:
                continue
            ins = fn(eng)
            if post is not None:
                ins.then_inc(post[1], 1 if post[0] == "inc" else 16)


def build_nc(upto=99, dbg=()):
    nc = bass.Bass("TRN2", target_bir_lowering=False, dynamic_dma_scratch_size=4096)
    dr = {}

    def din(name, shape, dt=F32):
        dr[name] = nc.dram_tensor(name, list(shape), dt, kind="ExternalInput").ap()
        return dr[name]

    x_d = din("x", [S, D])
    pp_d = din("pp", [NOWN, 256])
    pos_d = din("pos", [128, 16], I32)
    invf_d = din("invf", [128, 8])
    gcols_d = din("gcols", [128, 3, 16])
    gfin_d = din("gfin", [128, D])
    grep_d = din("grep", [3, 128, D])
    lnrep_d = din("lnrep", [128, 2, 1024])
    gsub_d = din("gsub", [128, 128])
    lamv_d = din("lamv", [128, 4, 64])
    wsT_d = din("wsT", [128, 8, 128])
    bsT_d = din("bsT", [128, 8])
    cw_d = din("cw", [128, 88, 4])
    w_in_d = din("w_in", [D, 5120])
    w_out_d = din("w_out", [D, D])
    w_up_d = din("w_up", [D, 2 * DFF])
    w_down_d = din("w_down", [DFF, D])
    w_gate_d = din("w_gate", [D, D])
    w_pleup_d = din("w_pleup", [256, D])
    y_d = nc.dram_tensor("y", [NOWN, D], F32, kind="ExternalOutput").ap()
    dbg_d = {}
    dbg_shapes = {"aT": ([128, 16 * 1152], BF16), "qT": ([128, 8 * 1026], BF16), "kT": ([128, 8 * 2048], BF16),
                  "Vext": ([128, 16 * 8 * 130], BF16), "gu": ([128, 9 * 1024], BF16), "vn": ([128, 9 * 1024], BF16),
                  "mix": ([128, 16 * 1026], BF16), "resid": ([128, 9 * 2048], F32), "hnT": ([128, 16 * 1026], BF16),
                  "cs": ([128, 256], F32)}
    for nm in dbg:
        shp, dt = dbg_shapes[nm]
        dbg_d[nm] = nc.dram_tensor("dbg_" + nm, shp, dt, kind="ExternalOutput").ap()

    TOTAL = 225000 // 4
    with ExitStack() as es:
        big = es.enter_context(nc.sbuf_tensor("big", [128, TOTAL], F32))
        banks = [es.enter_context(nc.psum_tensor("ps%d" % i, [128, 512], F32)) for i in range(8)]
        sems = [es.enter_context(nc.semaphore("s%d" % i)) for i in range(100)]
        P = Plan(sems)
        off = [0]

        def carve(nbytes):
            o = off[0]
            off[0] += (nbytes + 31) // 32 * 32
            assert off[0] <= TOTAL * 4, off[0]
            return o

        def V(o, dt, *dims, parts=128):
            n = 1
            for d_ in dims:
                n *= d_
            if dt == F32 or dt == I32:
                ap = big[0:parts, o // 4:o // 4 + n]
                if dt == I32:
                    ap = ap.bitcast(I32)
            else:
                ap = big[0:parts, o // 4:o // 4 + (n + 1) // 2].bitcast(BF16)
                if n % 2:
                    ap = ap[:, 0:n]
            if len(dims) == 2:
                ap = ap.rearrange("p (a b) -> p a b", a=dims[0])
            elif len(dims) == 3:
                ap = ap.rearrange("p (a b c) -> p a b c", a=dims[0], b=dims[1])
            return ap

        Z1 = carve(36864)
        Z2 = carve(82560)
        Z3 = carve(36864)
        Z4 = carve(16384)
        WT0 = carve(16384)
        WT1 = carve(16384)
        aT = V(Z1, BF16, 16, 1152)
        mixT = V(Z1, BF16, 16, 1026)
        hnT = V(Z1, BF16, 16, 1026)
        qT = V(Z2, BF16, 8, 1026)
        kT = V(Z2 + 16416, BF16, 8, 2048)
        Vext = V(Z2 + 49184, BF16, 16, 8, 130)
        resid = V(Z2, F32, 8, 2048)
        hres = V(Z2 + 65536, F32, 2048)
        sg = V(Z2 + 73728, BF16, 4, 1024)
        gu = V(Z3, BF16, 9, 1024)
        vn = V(Z3 + 18432, BF16, 9, 1024)
        XT = [V(Z3 + i * 8192, F32, 2048) for i in range(2)]
        XB = [V(Z3 + 16384 + i * 4096, BF16, 2048) for i in range(2)]
        grep = V(Z3 + 24576, F32, 2048)
        Pbuf = V(Z3, BF16, 2, 16, 384)
        actT = V(Z3, BF16, 8, 1024)
        hfull = [V(Z3 + 16384 + i * 4104, F32, 1026) for i in range(2)]
        accb = [V(Z3 + 16384 + 8208 + i * 4096, F32, 1024) for i in range(2)]
        OT = [V(Z3 + i * 8192, F32, 2048) for i in range(2)]
        lnrep = V(Z4, F32, 2, 1024)
        tmpf = [V(Z4 + 8192 + i * 2048, F32, 512) for i in range(4)]
        tmpN = V(Z4 + 8192, F32, 1024)
        qtok = [V(Z4 + i * 1024, BF16, 512) for i in range(2)]
        ropet = [V(Z4 + 2048 + i * 256, F32, 8, 8) for i in range(4)]
        qfs = [V(Z4 + 4096 + i * 2048, F32, 512) for i in range(2)]
        gmT = V(Z4, BF16, 1024)
        otok = V(Z4, BF16, 3, 128)
        o0 = [V(Z4 + 1024 + i * 512, F32, 128) for i in range(2)]
        o1 = [V(Z4 + 2048 + i * 512, F32, 128) for i in range(2)]
        ojunk = V(Z4 + 3072, F32, 128)
        wpu = V(Z4, BF16, 2, 2048)
        pT = V(Z4 + 8192, BF16, 2, 1024)
        ptile = V(Z4 + 12288, F32, 256)
        pbt = V(Z4 + 13312, BF16, 256)
        gfin = V(Z4, F32, 2048)
        WTR = V(WT0, BF16, 16384)
        assert WT1 == WT0 + 16384
        identf = V(carve(512), F32, 128)
        identb = V(carve(256), BF16, 128)
        oneb = V(carve(64), BF16, 16)
        gcols = V(carve(192), F32, 3, 16)
        gsub = V(carve(512), F32, 128)
        gsub8 = V(carve(512), F32, 128)
        lamv = V(carve(1024), F32, 4, 64)
        lamt = V(carve(512), F32, 2, 64)
        lams = V(carve(32), F32, 8)
        neglam = lams[:, 4:5]
        mhalf = V(carve(32), F32, 8)
        wsTb = V(carve(2048), BF16, 8, 128)
        bsT = V(carve(32), F32, 8)
        cw = V(carve(1408), F32, 88, 4)
        posi = V(carve(64), I32, 16)
        posf = V(carve(64), F32, 16)
        invf = V(carve(32), F32, 8)
        ang = V(carve(512), F32, 16, 8)
        angk = V(carve(512), F32, 16, 8)
        angi = V(carve(512), I32, 16, 8)
        cosT = V(carve(512), F32, 16, 8)
        sinT = V(carve(512), F32, 16, 8)
        halot = V(carve(2048), BF16, 8, 128)
        stat = V(carve(1024), F32, 256)
        bnst = V(carve(9 * 2 * 6 * 4), F32, 9, 2, 6)
        bnmv = V(carve(9 * 2 * 4), F32, 9, 2)

        Rb = [Reg("bank%d" % i) for i in range(8)]
        bankbf = [b[:, :].bitcast(BF16) for b in banks]
        DS_small = P.new_dsem()
        DS_w = [P.new_dsem() for _ in range(4)]
        for d_ in DS_w:
            d_.no_barrier = True
        DS_x = [P.new_dsem(), P.new_dsem()]
        DS_o = [P.new_dsem(), P.new_dsem()]
        DS_misc = P.new_dsem()
        DS_grep = P.new_dsem()
        DS_r = [P.new_dsem() for _ in range(9)]
        DS_wpu = P.new_dsem()
        DS_pt = P.new_dsem()
        R_w = [Reg("wq%d" % i) for i in range(4)]
        R_const = Reg("const")

        P.op("pool", lambda e: e.memset(identf, 0.0), writes=[R_const])
        P.op("pool", lambda e: e.affine_select(out=identf, in_=identf, pattern=[[-1, 128]], compare_op=ALU.not_equal,
                                               fill=1.0, base=0, channel_multiplier=1), writes=[R_const])
        P.op("dve", lambda e: e.tensor_copy(out=identb, in_=identf), reads=[R_const], writes=[R_const])
        P.op("dve", lambda e: e.memset(oneb, 1.0), writes=[R_const])
        P.op("dve", lambda e: e.memset(mhalf, -0.5), writes=[R_const])
        for dst, src in ((gcols, gcols_d), (gsub, gsub_d), (lamv, lamv_d), (bsT, bsT_d), (cw, cw_d), (posi, pos_d),
                         (invf, invf_d)):
            P.dma("sp", (lambda d_, s_: lambda e: e.dma_start(out=d_, in_=s_))(dst, src), DS_small, writes=[R_const])
        P.dma("pool", lambda e: e.dma_start(out=wsTb, in_=wsT_d), DS_wpu, writes=[R_const])
        P.barrier()
        for i in range(2):
            P.op("dve", (lambda i: lambda e: e.tensor_tensor(out=lamt[:, i, :], in0=lamv[:, 2 * i, :], in1=lamv[:, 2 * i + 1, :],
                                                             op=ALU.mult))(i), reads=[R_const], writes=[R_const])
            P.op("dve", (lambda i: lambda e: e.tensor_reduce(out=lams[:, i:i + 1], in_=lamt[:, i, :], axis=AX.X, op=ALU.add))(i),
                 reads=[R_const], writes=[R_const])
        P.op("act", lambda e: e.activation(out=lams[:, 2:4], in_=lams[:, 0:2], func=AF.Exp), reads=[R_const], writes=[R_const])
        P.op("dve", lambda e: e.tensor_tensor(out=lams[:, 4:5], in0=lams[:, 3:4], in1=lams[:, 2:3], op=ALU.subtract),
             reads=[R_const], writes=[R_const])
        P.op("dve", lambda e: e.tensor_scalar(out=lams[:, 4:5], in0=lams[:, 4:5], scalar1=-0.2, scalar2=None, op0=ALU.add),
             reads=[R_const], writes=[R_const])
        P.op("dve", lambda e: e.tensor_scalar(out=gsub8, in0=gsub, scalar1=0.8, scalar2=None, op0=ALU.mult),
             reads=[R_const], writes=[R_const])
        P.op("dve", lambda e: e.tensor_copy(out=posf, in_=posi), reads=[R_const], writes=[R_const])
        P.op("dve", lambda e: e.tensor_tensor(out=ang, in0=posf.unsqueeze(2).broadcast_to([128, 16, 8]),
                                              in1=invf.unsqueeze(1).broadcast_to([128, 16, 8]), op=ALU.mult),
             reads=[R_const], writes=[R_const])
        for dstT, shift in ((sinT, 0.0), (cosT, math.pi / 2)):
            P.op("dve", (lambda sh: lambda e: e.tensor_scalar(out=angk, in0=ang, scalar1=sh, scalar2=1.0 / TWO_PI, op0=ALU.add,
                                                              op1=ALU.mult))(shift), reads=[R_const], writes=[R_const])
            P.op("dve", lambda e: e.tensor_copy(out=angi, in_=angk), reads=[R_const], writes=[R_const])
            P.op("dve", lambda e: e.tensor_copy(out=angk, in_=angi), reads=[R_const], writes=[R_const])
            P.op("dve", lambda e: e.tensor_scalar(out=angk, in0=angk, scalar1=-TWO_PI, scalar2=None, op0=ALU.mult),
                 reads=[R_const], writes=[R_const])
            P.op("dve", (lambda sh: lambda e: e.scalar_tensor_tensor(out=angk, in0=ang, scalar=sh, in1=angk, op0=ALU.add,
                                                                     op1=ALU.add))(shift), reads=[R_const], writes=[R_const])
            P.op("dve", (lambda d_: lambda e: e.tensor_scalar(out=d_, in0=angk, scalar1=math.pi, scalar2=-TWO_PI, op0=ALU.is_gt,
                                                              op1=ALU.mult))(dstT), reads=[R_const], writes=[R_const])
            P.op("dve", (lambda d_: lambda e: e.tensor_tensor(out=angk, in0=angk, in1=d_, op=ALU.add))(dstT),
                 reads=[R_const], writes=[R_const])
            P.op("dve", (lambda d_: lambda e: e.tensor_scalar(out=d_, in0=angk, scalar1=-math.pi, scalar2=TWO_PI, op0=ALU.is_lt,
                                                              op1=ALU.mult))(dstT), reads=[R_const], writes=[R_const])
            P.op("dve", (lambda d_: lambda e: e.tensor_tensor(out=angk, in0=angk, in1=d_, op=ALU.add))(dstT),
                 reads=[R_const], writes=[R_const])
            P.op("dve", lambda e: e.tensor_scalar(out=angk, in0=angk, scalar1=-3.14159, scalar2=3.14159, op0=ALU.max, op1=ALU.min),
                 reads=[R_const], writes=[R_const])
            P.op("act", (lambda d_: lambda e: e.activation(out=d_, in_=angk, func=AF.Sin))(dstT), reads=[R_const], writes=[R_const])
        P.barrier()

        st_i = [0]

        def stcol(n=1):
            i = st_i[0]
            if i + n > 256:
                i = 0
            st_i[0] = i + n
            return stat[:, i:i + n]

        def rsqrt_col(src_col, mult, R_s, parts=128, lnexp=False):
            v = stcol()
            r = stcol()
            P.op("dve", lambda e: e.tensor_scalar(out=v[0:parts], in0=src_col, scalar1=mult, scalar2=EPS, op0=ALU.mult, op1=ALU.add),
                 reads=[R_s], writes=[R_s])
            if lnexp:
                P.op("act", lambda e: e.activation(out=v[0:parts], in_=v[0:parts], func=AF.Ln), reads=[R_s], writes=[R_s])
                P.op("act", lambda e: e.activation(out=r[0:parts], in_=v[0:parts], func=AF.Exp, scale=-0.5), reads=[R_s], writes=[R_s])
            else:
                P.op("act", lambda e: e.activation(out=v[0:parts], in_=v[0:parts], func=AF.Sqrt), reads=[R_s], writes=[R_s])
                P.op("dve", lambda e: e.reciprocal(out=r[0:parts], in_=v[0:parts]), reads=[R_s], writes=[R_s])
            return r[0:parts]

        xs_i = [0]
        bk_i = [0]

        R_grep = Reg("grep")

        def load_grep(gi):
            P.dma("sp", lambda e: e.dma_start(out=grep, in_=grep_d[gi]), DS_grep, writes=[R_grep])

        def norm_A(src, R_src):
            s = xs_i[0] % 2
            xs_i[0] += 1
            R_s = Reg()
            ss = stcol()
            xb = XB[s]
            R_xb = R_XB[s]
            P.op("act", lambda e: e.activation(out=xb, in_=src, func=AF.Square, accum_out=ss), reads=[R_src], writes=[R_xb, R_s])
            r = rsqrt_col(ss, 1.0 / D, R_s)
            P.op("dve", lambda e: e.scalar_tensor_tensor(out=xb, in0=src, scalar=r, in1=grep, op0=ALU.mult, op1=ALU.mult),
                 reads=[R_src, R_s, R_grep], writes=[R_xb])
            return xb, R_xb

        def norm_B(st, dstT, tcol, R_dst, bankset):
            xb, R_xb = st
            for half in range(2):
                b = bankset[bk_i[0] % len(bankset)]
                bk_i[0] += 1
                for c in range(8):
                    kc = half * 8 + c
                    P.op("pe", lambda e: e.transpose(out=bankbf[b][:, c * 128:(c + 1) * 128],
                                                     in_=xb[:, kc * 128:(kc + 1) * 128], identity=identb),
                         reads=[R_xb, R_const], writes=[Rb[b]], inc=(c == 7))
                dst = dstT[:, half * 8:(half + 1) * 8, tcol:tcol + 128]
                src_ps = bankbf[b][:, :].rearrange("p (c t) -> p c t", c=8)
                if half == 0:
                    P.op("act", lambda e: e.activation(out=dst, in_=src_ps, func=AF.Copy), reads=[Rb[b]], writes=[R_dst])
                else:
                    P.op("dve", lambda e: e.tensor_copy(out=dst, in_=src_ps), reads=[Rb[b]], writes=[R_dst])

        def norm_to_T(src, R_src, gi, dstT, tcol, R_dst, bankset):
            norm_B(norm_A(src, R_src), dstT, tcol, R_dst, bankset)

        def norm_pipeline(items, bankset=(0, 1, 2, 3)):
            prev = None
            for (src_fn, dstT, tcol, R_dst, post) in items:
                src, R_src = src_fn()
                st = norm_A(src, R_src)
                if prev is not None:
                    norm_B(prev[0], prev[1], prev[2], prev[3], list(bankset))
                    if prev[4] is not None:
                        prev[4]()
                prev = (st, dstT, tcol, R_dst, post)
            norm_B(prev[0], prev[1], prev[2], prev[3], list(bankset))
            if prev[4] is not None:
                prev[4]()

        R_XT = [Reg(), Reg()]
        R_XB = [Reg(), Reg()]
        wslot = [0]
        lb_i = [0]

        tasks = []

        def add_task(fn, spec=None):
            tasks.append((spec, fn, P.skip))

        def section(fn):
            def f(s_):
                flush_deferred()
                fn()
            add_task(f)
            return fn

        def wview(spec, q0):
            wd, k0, nk, c0, ncols = spec
            return WTR[:, q0 * 4096:q0 * 4096 + nk * ncols].rearrange("p (k n) -> p k n", k=nk)

        def wquarters(spec):
            return (spec[2] * spec[4] * 2 + 8191) // 8192

        def wregs(spec, q0):
            return [R_w[q0 + i] for i in range(wquarters(spec))]

        def issue_weights(spec, q0):
            wd, k0, nk, c0, ncols = spec
            wv = wview(spec, q0)
            step = max(1, 2048 // ncols)
            g = 0
            while g < nk:
                n = min(step, nk - g)
                q = q0 + (g * ncols * 2) // 8192
                P.dma("pool", lambda e: e.dma_start(
                    out=wv[:, g:g + n, :],
                    in_=wd[(k0 + g) * 128:(k0 + g + n) * 128, c0:c0 + ncols].rearrange("(k p) n -> p k n", p=128)),
                      DS_w[q], writes=[R_w[q]])
                g += n

        def linear_tm(wd, k0, nk, c0, tiles, lhs_fn, epi, bankset, ncols=512):
            tiles = list(tiles)

            spec = (wd, k0, nk, c0, ncols)

            def fn(q0):
                wv = wview(spec, q0)
                wr = wregs(spec, q0)
                for t in tiles:
                    b = bankset[lb_i[0] % len(bankset)]
                    lb_i[0] += 1
                    for kc in range(nk):
                        lhs, R_l = lhs_fn(kc, t)
                        M = lhs.shape[-1]
                        P.op("pe", lambda e: e.matmul(banks[b][0:M, 0:ncols], lhsT=lhs, rhs=wv[:, kc, :],
                                                      start=(kc == 0), stop=(kc == nk - 1)),
                             reads=wr + [R_l], writes=[Rb[b]], inc=(kc == nk - 1))
                    flush_deferred()
                    d_ = epi(t, b)
                    if d_ is not None:
                        deferred.append(d_)
            add_task(fn, spec)

        deferred = []

        def flush_deferred():
            while deferred:
                deferred.pop(0)()

        def barrier_task():
            def f(s_):
                flush_deferred()
                P.barrier()
            add_task(f)

        def load_x_tile(row0):
            s = xs_i[0] % 2
            P.dma("sp", lambda e: e.dma_start(out=XT[s], in_=x_d[row0:row0 + 128, :]), DS_x[s], writes=[R_XT[s]])
            return XT[s], R_XT[s]

        R_aT = [Reg("aT%d" % i) for i in range(9)]
        R_qT = Reg("qT")
        R_kT = Reg("kT")
        R_V = Reg("V")
        R_gu = [Reg() for _ in range(9)]
        R_vn = [Reg() for _ in range(9)]
        R_mixA = Reg("mixA")
        R_mixG = Reg("mixG")
        R_tmp = [Reg() for _ in range(4)]
        R_qtok = [Reg(), Reg()]
        R_rope = Reg()
        R_lnrep = Reg()
        R_qf = [Reg(), Reg()]
        R_halot = Reg()
        R_resid = [Reg("resid%d" % i) for i in range(9)]
        R_hnT = [Reg() for _ in range(9)]
        tmp_i = [0]

        def rope_epi(dstT, R_dst, hh, colfn):
            def epi(t, b):
                s = tmp_i[0] % 2
                tmp_i[0] += 1
                qt = qtok[s]
                bank3 = banks[b][:, :].rearrange("p (g d) -> p g d", d=64)
                qt3 = qt.rearrange("p (g d) -> p g d", d=64)
                cosb = cosT[:, t, :].unsqueeze(1).broadcast_to([128, 8, 8])
                sinb = sinT[:, t, :].unsqueeze(1).broadcast_to([128, 8, 8])
                qf = qfs[s]
                qf3 = qf.rearrange("p (g d) -> p g d", d=64)
                P.op("act", lambda e: e.activation(out=qf, in_=banks[b][:, :], func=AF.Copy), reads=[Rb[b]], writes=[R_qf[s]])
                P.op("act", lambda e: e.activation(out=qt, in_=banks[b][:, :], func=AF.Copy), reads=[Rb[b]], writes=[R_qtok[s]])
                P.op("dve", lambda e: e.tensor_tensor(out=ropet[0], in0=qf3[:, :, 0:8], in1=cosb, op=ALU.mult),
                     reads=[R_qf[s], R_const], writes=[R_rope])
                P.op("dve", lambda e: e.tensor_tensor(out=ropet[1], in0=qf3[:, :, 8:16], in1=sinb, op=ALU.mult),
                     reads=[R_qf[s], R_const], writes=[R_rope])
                P.op("dve", lambda e: e.tensor_tensor(out=qt3[:, :, 0:8], in0=ropet[0], in1=ropet[1], op=ALU.subtract),
                     reads=[R_rope], writes=[R_qtok[s]])
                P.op("dve", lambda e: e.tensor_tensor(out=ropet[2], in0=qf3[:, :, 8:16], in1=cosb, op=ALU.mult),
                     reads=[R_qf[s], R_const], writes=[R_rope])
                P.op("dve", lambda e: e.tensor_tensor(out=ropet[3], in0=qf3[:, :, 0:8], in1=sinb, op=ALU.mult),
                     reads=[R_qf[s], R_const], writes=[R_rope])
                P.op("dve", lambda e: e.tensor_tensor(out=qt3[:, :, 8:16], in0=ropet[2], in1=ropet[3], op=ALU.add),
                     reads=[R_rope], writes=[R_qtok[s]])
                tb = 4 + (tmp_i[0] % 2)

                def part2():
                    rope_part2(qt, s, tb, t)
                return part2

            def rope_part2(qt, s, tb, t):
                for j in range(4):
                    P.op("pe", (lambda j: lambda e: e.transpose(out=bankbf[tb][:, j * 128:(j + 1) * 128],
                                                                in_=qt[:, j * 128:(j + 1) * 128], identity=identb))(j),
                         reads=[R_qtok[s], R_const], writes=[Rb[tb]], inc=(j == 3))
                col, n = colfn(t)
                if n == 128:
                    P.op("dve", lambda e: e.tensor_copy(out=dstT[:, 4 * hh:4 * hh + 4, col:col + n],
                                                        in_=bankbf[tb][:, 0:512].rearrange("p (j c) -> p j c", j=4)),
                         reads=[Rb[tb]], writes=[R_dst])
                else:
                    P.op("dve", lambda e: e.tensor_copy(out=halot[:, 0:4, :], in_=bankbf[tb][:, 0:512].rearrange("p (j c) -> p j c", j=4)),
                         reads=[Rb[tb]], writes=[R_halot])
                    P.op("dve", lambda e: e.tensor_copy(out=dstT[:, 4 * hh:4 * hh + 4, col:col + n], in_=halot[:, 0:4, 0:n]),
                         reads=[R_halot], writes=[R_dst])
            return epi

        def v_epi(hh):
            def epi(t, b):
                P.op("act", lambda e: e.activation(out=Vext[:, t, 4 * hh:4 * hh + 4, 0:128],
                                                   in_=banks[b][:, :].rearrange("p (j d) -> p j d", j=4), func=AF.Copy),
                     reads=[Rb[b]], writes=[R_V])
            return epi

        own_col = lambda t: (t * 128, 128) if t < 8 else (1024, 1)
        lin_banks = [0, 1, 2, 3]

        P.skip = upto < 1
        R_aTp = R_aT

        @section
        def sec_p1a():
            P.op("pool", lambda e: e.memset(Vext[:, :, :, 128:130], 1.0), writes=[R_V])
            load_grep(0)
            norm_pipeline([((lambda t=8 + i: load_x_tile(t * 128)), aT, i * 128, R_aTp[i], None) for i in range(8)])
        for hh in range(2):
            linear_tm(w_in_d, 0, 16, 1024 + hh * 512, range(8), lambda kc, i: (aT[:, kc, i * 128:(i + 1) * 128], R_aTp[i]),
                      (lambda ep: lambda i, b: ep(8 + i, b))(rope_epi(kT, R_kT, hh, lambda t: (t * 128, 128))), lin_banks)
        for hh in range(2):
            linear_tm(w_in_d, 0, 16, 2048 + hh * 512, range(8), lambda kc, i: (aT[:, kc, i * 128:(i + 1) * 128], R_aTp[i]),
                      (lambda ep: lambda i, b: ep(8 + i, b))(v_epi(hh)), lin_banks)

        P.skip = upto < 2
        @section
        def sec_p1b():
            norm_pipeline([((lambda t=t: load_x_tile(t * 128)), aT, t * 128, R_aT[t], None) for t in range(9)])
        barrier_task()

        @section
        def sec_ln_load():
            P.dma("sp", lambda e: e.dma_start(out=lnrep, in_=lnrep_d), DS_misc, writes=[R_lnrep])
        own_lhs = lambda kc, t: (aT[:, kc, t * 128:(t + 1) * 128], R_aT[t])

        def gu_epi(hcb):
            def epi(t, b):
                P.op("act", lambda e: e.activation(out=gu[:, t, hcb * 512:(hcb + 1) * 512], in_=banks[b][:, :], func=AF.Gelu),
                     reads=[Rb[b]], writes=[R_gu[t]])
            return epi

        def gv_epi(hcb):
            def epi(t, b):
                s = 2 + (tmp_i[0] % 2)
                tmp_i[0] += 1
                tf = tmpf[s]
                P.op("act", lambda e: e.activation(out=tf, in_=banks[b][:, :], func=AF.Gelu), reads=[Rb[b]], writes=[R_tmp[s]])
                P.op("dve", lambda e: e.bn_stats(out=bnst[:, t, hcb, :], in_=tf), reads=[R_tmp[s]], writes=[R_vn[t]])
                P.op("act", lambda e: e.activation(out=vn[:, t, hcb * 512:(hcb + 1) * 512], in_=tf, func=AF.Copy), reads=[R_tmp[s]],
                     writes=[R_vn[t]])
                if hcb == 1:
                    P.op("dve", lambda e: e.bn_aggr(out=bnmv[:, t, :], in_=bnst[:, t, :, :].rearrange("p a b -> p (a b)")),
                         reads=[R_vn[t]], writes=[R_vn[t]])
                    r = rsqrt_col(bnmv[:, t, 1:2], 1.0, R_vn[t])
                    P.op("dve", lambda e: e.scalar_tensor_tensor(out=tmpN, in0=vn[:, t, :], scalar=bnmv[:, t, 0:1], in1=lnrep[:, 0, :],
                                                                 op0=ALU.subtract, op1=ALU.mult),
                         reads=[R_vn[t], R_lnrep], writes=[R_tmp[0], R_tmp[1]])
                    P.op("dve", lambda e: e.scalar_tensor_tensor(out=vn[:, t, :], in0=tmpN, scalar=r, in1=lnrep[:, 1, :],
                                                                 op0=ALU.mult, op1=ALU.add),
                         reads=[R_lnrep, R_tmp[0], R_tmp[1]], writes=[R_vn[t]])
            return epi

        for hcb in range(2):
            linear_tm(w_in_d, 0, 16, 3072 + hcb * 512, range(9), own_lhs, gu_epi(hcb), lin_banks)
        for hcb in range(2):
            linear_tm(w_in_d, 0, 16, 4096 + hcb * 512, range(9), own_lhs, gv_epi(hcb), lin_banks)

        P.skip = upto < 3
        sp_i = [0]

        @section
        def sec_p3():
            for t in range(9):
                for j in range(2):
                    b = 4 + (sp_i[0] % 4)
                    sp_i[0] += 1
                    for hq in range(4):
                        h = 4 * j + hq
                        P.op("pe", lambda e: e.matmul(banks[b][:, hq * 128:(hq + 1) * 128], lhsT=wsTb[:, h, :],
                                                      rhs=vn[:, t, h * 128:(h + 1) * 128], start=True, stop=True),
                             reads=[R_vn[t], R_const], writes=[Rb[b]], inc=(hq == 3))
                    s = tmp_i[0] % 2
                    tmp_i[0] += 1
                    tf = tmpf[s]
                    P.op("dve", lambda e: e.tensor_tensor(
                        out=tf.rearrange("p (j d) -> p j d", j=4), in0=banks[b][:, :].rearrange("p (j d) -> p j d", j=4),
                        in1=bsT[:, 4 * j:4 * j + 4].unsqueeze(2).broadcast_to([128, 4, 128]), op=ALU.add),
                         reads=[Rb[b], R_const], writes=[R_tmp[s]])
                    P.op("pool", lambda e: e.tensor_tensor(out=gu[:, t, j * 512:(j + 1) * 512], in0=tf,
                                                           in1=gu[:, t, j * 512:(j + 1) * 512], op=ALU.mult),
                         reads=[R_tmp[s]], writes=[R_gu[t]])
        barrier_task()

        @section
        def sec_qpad():
            P.op("pool", lambda e: e.memset(qT[:, :, 1025:1026], 0.0), writes=[R_qT])
        for hh in range(2):
            linear_tm(w_in_d, 0, 16, hh * 512, range(9), own_lhs, rope_epi(qT, R_qT, hh, own_col), lin_banks)
        for hh in range(2):
            linear_tm(w_in_d, 0, 16, 1024 + hh * 512, range(8), own_lhs, rope_epi(kT, R_kT, hh, lambda t: (t * 128, 128)), lin_banks)
        for hh in range(2):
            linear_tm(w_in_d, 0, 16, 2048 + hh * 512, range(8), own_lhs, v_epi(hh), lin_banks)
        barrier_task()

        @section
        def sec_gmT():
            P.op("pool", lambda e: e.memset(mixT[:, :, 1025:1026], 0.0), writes=[R_mixG, R_mixA])
            for t in range(9):
                tb = 4 + (t % 2)
                for c in range(8):
                    P.op("pe", lambda e: e.transpose(out=bankbf[tb][:, c * 128:(c + 1) * 128],
                                                     in_=gu[:, t, c * 128:(c + 1) * 128], identity=identb),
                         reads=[R_gu[t], R_const], writes=[Rb[tb]], inc=(c == 7))
                col, n = own_col(t)
                if n == 128:
                    P.op("dve", lambda e: e.tensor_copy(out=mixT[:, 8:16, col:col + n], in_=bankbf[tb][:, :].rearrange("p (c t) -> p c t", c=8)),
                         reads=[Rb[tb]], writes=[R_mixG])
                else:
                    P.op("dve", lambda e: e.tensor_copy(out=halot, in_=bankbf[tb][:, :].rearrange("p (c t) -> p c t", c=8)),
                         reads=[Rb[tb]], writes=[R_halot])
                    P.op("dve", lambda e: e.tensor_copy(out=mixT[:, 8:16, col:col + n], in_=halot[:, :, 0:n]), reads=[R_halot], writes=[R_mixG])
        barrier_task()

        @section
        def sec_dbg1():
            for nm, ap in (("qT", qT), ("kT", kT), ("gu", gu), ("vn", vn)):
                if nm in dbg:
                    P.dma("sp", lambda e: e.dma_start(out=dbg_d[nm], in_=ap.rearrange("p a b -> p (a b)")), DS_misc)
            if "Vext" in dbg:
                P.dma("sp", lambda e: e.dma_start(out=dbg_d["Vext"], in_=Vext.rearrange("p a b c -> p (a b c)")), DS_misc)
            if "cs" in dbg:
                P.dma("sp", lambda e: e.dma_start(out=dbg_d["cs"][:, 0:128], in_=cosT.rearrange("p a b -> p (a b)")), DS_misc)
                P.dma("sp", lambda e: e.dma_start(out=dbg_d["cs"][:, 128:256], in_=sinT.rearrange("p a b -> p (a b)")), DS_misc)
            if dbg:
                P.barrier()

        P.skip = upto < 4
        Pbuf2 = [V(Z3 + i * 16512, BF16, 2, 16, 258) for i in range(2)]
        qm1 = V(Z3 + 33024, BF16, 8, 1026)
        otok4 = V(Z4 + 12608, BF16, 3, 128)
        o0b = [V(Z4 + 13376 + i * 512, F32, 128) for i in range(2)]
        o1b = [V(Z4 + 14400 + i * 512, F32, 128) for i in range(2)]
        ojunk4 = V(Z4 + 15424, F32, 128)

        @section
        def sec_p4():
            R_P = [[[Reg() for _ in range(16)] for _ in range(2)] for _ in range(2)]
            R_otok = Reg()
            R_o = [Reg(), Reg()]
            qblocks = [(0, 256), (256, 512), (512, 768), (768, 1025)]
            blocks = [(h, c0, c1) for h in range(8) for (c0, c1) in qblocks]
            s_i = [0]
            o_i = [0]
            R_qm = Reg()
            R_qm2 = Reg()
            P.op("dve", lambda e: e.memset(qm1[0:64], 0.0), writes=[R_qm])
            P.op("act", lambda e: e.activation(out=qm1[64:128, 0:4, :], in_=qT[64:128, 0:4, :], func=AF.Copy), reads=[R_qT], writes=[R_qm2])
            P.op("dve", lambda e: e.tensor_copy(out=qm1[64:128, 4:8, :], in_=qT[64:128, 4:8, :]), reads=[R_qT], writes=[R_qm])
            P.op("dve", lambda e: e.memset(qT[64:128], 0.0), reads=[R_qm2], writes=[R_qT])
            P.op("dve", lambda e: e.tensor_copy(out=stcol(), in_=mhalf[:, 0:1]), reads=[R_qm2], writes=[R_qm])
            qsrc = [qT, qm1]
            R_qs = [R_qT, R_qm]

            def emit_qk(bi, kc):
                h, c0, c1 = blocks[bi]
                n = c1 - c0
                pb = Pbuf2[bi % 2]
                sb = (s_i[0] % 2) * 2
                s_i[0] += 1
                for c in range(2):
                    P.op("pe", lambda e: e.matmul(banks[sb + c][:, 0:n], lhsT=kT[:, h, kc * 128:(kc + 1) * 128],
                                                  rhs=qsrc[c][:, h, c0:c1], start=True, stop=True),
                         reads=[R_kT, R_qs[c]], writes=[Rb[sb + c]])
                for c in range(2):
                    P.op("act", lambda e: e.activation(out=pb[:, c, kc, 0:n], in_=banks[sb + c][:, 0:n], func=AF.Exp, scale=0.125),
                         reads=[Rb[sb + c]], writes=[R_P[bi % 2][c][kc]])

            def av_items(bi):
                h, c0, c1 = blocks[bi]
                n = c1 - c0
                pb = Pbuf2[bi % 2]
                RP = R_P[bi % 2]
                ntile = (n + 127) // 128
                tb = 6 + (bi % 2)
                items = []
                e2 = []
                for j in range(ntile):
                    m = 128
                    jc0 = j * 128 if (j + 1) * 128 <= n else n - 128
                    ob = 4 + (o_i[0] % 2)
                    o_i[0] += 1
                    s = o_i[0] % 2
                    R_s = R_o[s]
                    O = banks[ob]
                    st = {}

                    def mk_av(c, j=j, m=m, ob=ob, jc0=jc0):
                        def f():
                            for kc in range(16):
                                P.op("pe", lambda e: e.matmul(banks[ob][0:m, c * 130:c * 130 + 129], lhsT=pb[:, c, kc, jc0:jc0 + m],
                                                              rhs=Vext[:, kc, h, 0:129], start=(kc == 0), stop=(kc == 15)),
                                     reads=[RP[c][kc], R_V], writes=[Rb[ob]], inc=(kc == 15 and c == 1))
                        return f

                    def e1(j=j, m=m, ob=ob, s=s, R_s=R_s, O=O, st=st):
                        rr = stcol(2)
                        P.op("dve", lambda e: e.reciprocal(out=rr[0:m, 0:1], in_=O[0:m, 128:129]), reads=[Rb[ob]], writes=[R_s])
                        P.op("dve", lambda e: e.reciprocal(out=rr[0:m, 1:2], in_=O[0:m, 258:259]), reads=[Rb[ob]], writes=[R_s])
                        P.op("dve", lambda e: e.tensor_tensor(out=rr[0:m, 1:2], in0=rr[0:m, 1:2], in1=neglam[0:m], op=ALU.mult),
                             reads=[R_const], writes=[R_s])
                        P.op("dve", lambda e: e.tensor_scalar(out=o0b[s][0:m], in0=O[0:m, 0:128], scalar1=rr[0:m, 0:1], scalar2=None,
                                                              op0=ALU.mult), reads=[Rb[ob]], writes=[R_s])
                        P.op("dve", lambda e: e.scalar_tensor_tensor(out=o1b[s][0:m], in0=O[0:m, 130:258], scalar=rr[0:m, 1:2],
                                                                     in1=o0b[s][0:m], op0=ALU.mult, op1=ALU.add),
                             reads=[Rb[ob]], writes=[R_s])
                        ssq = stcol()
                        P.op("dve", lambda e: e.scalar_tensor_tensor(out=ojunk4[0:m], in0=o1b[s][0:m], scalar=1.0, in1=o1b[s][0:m],
                                                                     op0=ALU.mult, op1=ALU.mult, accum_out=ssq[0:m]),
                             reads=[], writes=[R_s])
                        v = stcol()
                        P.op("dve", lambda e: e.tensor_scalar(out=v[0:m], in0=ssq[0:m], scalar1=1.0 / 128, scalar2=EPS, op0=ALU.mult,
                                                              op1=ALU.add), reads=[], writes=[R_s])
                        st["v"] = v

                    def e2f(m=m, R_s=R_s, st=st):
                        v = st["v"]
                        r = stcol()
                        P.op("act", lambda e: e.activation(out=v[0:m], in_=v[0:m], func=AF.Ln), reads=[R_s], writes=[R_s])
                        P.op("act", lambda e: e.activation(out=r[0:m], in_=v[0:m], func=AF.Exp, scale=-0.5), reads=[R_s], writes=[R_s])
                        st["r"] = r

                    def e3(j=j, m=m, s=s, R_s=R_s, st=st, last=(j == ntile - 1)):
                        r = st["r"]
                        P.op("dve", lambda e: e.scalar_tensor_tensor(out=otok4[0:m, j, :], in0=o1b[s][0:m], scalar=r[0:m], in1=gsub8[0:m],
                                                                     op0=ALU.mult, op1=ALU.mult), reads=[R_const, R_s], writes=[R_otok])
                        P.op("pe", lambda e: e.transpose(out=bankbf[tb][:, j * 128:j * 128 + m], in_=otok4[0:m, j, :],
                                                         identity=identb[0:m, 0:m]),
                             reads=[R_otok, R_const], writes=[Rb[tb]])
                        if last:
                            nm = (n // 128) * 128
                            P.op("dve", lambda e: e.tensor_copy(out=mixT[:, h, c0:c0 + nm], in_=bankbf[tb][:, 0:nm]),
                                 reads=[Rb[tb]], writes=[R_mixA])
                            if nm < n:
                                P.op("dve", lambda e: e.tensor_copy(out=mixT[:, h, c0 + nm:c0 + n], in_=bankbf[tb][:, nm + 127:nm + 128]),
                                     reads=[Rb[tb]], writes=[R_mixA])
                    items += [mk_av(0), mk_av(1), e1]
                    e2.append((e2f, e3))
                out = []
                k = 0
                for j in range(ntile):
                    out += items[3 * j:3 * j + 3]
                    if j >= 1:
                        out += [e2[j - 1][0], e2[j - 1][1]]
                out += [e2[ntile - 1][0]]
                out += [e2[ntile - 1][1]]
                return out

            nb = len(blocks)
            for bi in range(nb + 1):
                prev = av_items(bi - 1) if bi >= 1 else []
                npv = len(prev)
                done = 0
                for kc in range(16):
                    if bi < nb:
                        emit_qk(bi, kc)
                    want = (npv * (kc + 1) + 15) // 16
                    while done < want:
                        prev[done]()
                        done += 1
        barrier_task()

        @section
        def sec_dbgmix():
            if "mix" in dbg:
                P.dma("sp", lambda e: e.dma_start(out=dbg_d["mix"], in_=mixT.rearrange("p a b -> p (a b)")), DS_misc)
                P.barrier()

        P.skip = upto < 5
        @section
        def sec_xpre():
            for t in range(8):
                P.dma("sp", lambda e: e.dma_start(out=resid[:, t, :], in_=x_d[t * 128:(t + 1) * 128, :]), DS_r[t],
                      writes=[R_resid[t]])
            P.dma("sp", lambda e: e.dma_start(out=hres, in_=x_d[897:1025, :]), DS_r[8], writes=[R_resid[8]])

        def mix_lhs(kc, t):
            col = t * 128 if t < 8 else 897
            return mixT[:, kc, col:col + 128], (R_mixA if kc < 8 else R_mixG)

        def resid_add_epi(cb):
            def epi(t, b):
                if t < 8:
                    dst = resid[:, t, cb * 512:(cb + 1) * 512]
                    P.op("dve", lambda e: e.tensor_tensor(out=dst, in0=banks[b][:, :], in1=dst, op=ALU.add),
                         reads=[Rb[b]], writes=[R_resid[t]])
                else:
                    dst = hres[:, cb * 512:(cb + 1) * 512]
                    P.op("dve", lambda e: e.tensor_tensor(out=dst, in0=banks[b][:, :], in1=dst, op=ALU.add),
                         reads=[Rb[b]], writes=[R_resid[8]])
            return epi

        for cb in range(4):
            linear_tm(w_out_d, 0, 16, cb * 512, range(9), mix_lhs, resid_add_epi(cb), lin_banks)
        barrier_task()

        @section
        def sec_dbgres():
            if "resid" in dbg:
                P.dma("sp", lambda e: e.dma_start(out=dbg_d["resid"][:, 0:8 * 2048], in_=resid.rearrange("p a b -> p (a b)")), DS_misc)
                P.dma("sp", lambda e: e.dma_start(out=dbg_d["resid"][0:1, 8 * 2048:9 * 2048], in_=hres[96:97, :]), DS_misc)
                P.barrier()

        halo16 = V(Z4 + 8192, BF16, 16, 128)
        R_halo16 = Reg()

        def halo_norm(gi):
            norm_to_T(hres, R_resid[8], gi, halo16, 0, R_halo16, [0, 1, 2, 3])
            P.op("dve", lambda e: e.tensor_copy(out=hnT[:, :, 1024], in_=halo16[:, :, 127]), reads=[R_halo16], writes=[R_hnT[8]])

        P.skip = upto < 6

        @section
        def sec_p6():
            load_grep(1)
            norm_pipeline([((lambda t=t: (resid[:, t, :], R_resid[t])), hnT, t * 128, R_hnT[t], None) for t in range(8)])
            halo_norm(1)
        barrier_task()

        @section
        def sec_dbghn():
            if "hnT" in dbg:
                P.dma("sp", lambda e: e.dma_start(out=dbg_d["hnT"], in_=hnT.rearrange("p a b -> p (a b)")), DS_misc)
                P.barrier()

        P.skip = upto < 7
        R_hf = [Reg(), Reg()]
        R_acc = [Reg(), Reg()]
        R_sg = [Reg() for _ in range(4)]
        R_act = Reg()
        tblocks = [(0, 342), (342, 684), (684, 1025)]
        groups = [(0, 1, 2, 3), (4, 5, 6, 7), (8, 9, 10, 11), (12, 13, 14, 15), (16, 17, 18, 19), (20, 21)]
        f_i = [0]
        all_hn = list(R_hnT)

        @section
        def sec_p7init():
            P.op("pool", lambda e: e.memset(hfull[0][:, 0:1], 0.0), writes=[R_hf[0]])
            P.op("pool", lambda e: e.memset(hfull[1][:, 0:1], 0.0), writes=[R_hf[1]])

        def ffn_up_task(j, jl, sel, spec):
            def fn(q0):
                wv = wview(spec, q0)
                wr = wregs(spec, q0)
                for sub in range(2):
                    blk = sel * 44 + j * 2 + sub
                    hs = f_i[0] % 2
                    f_i[0] += 1
                    bset = [0, 1, 2] if hs == 0 else [3, 4, 5]
                    for ti, (c0, c1) in enumerate(tblocks):
                        b = bset[ti]
                        n = c1 - c0
                        for kc in range(16):
                            P.op("pe", lambda e: e.matmul(banks[b][:, 0:n], lhsT=wv[:, kc, sub * 128:(sub + 1) * 128],
                                                          rhs=hnT[:, kc, c0:c1], start=(kc == 0), stop=(kc == 15)),
                                 reads=wr + (all_hn if kc == 0 else []), writes=[Rb[b]], inc=(kc == 15))
                        P.op("act", lambda e: e.activation(out=hfull[hs][:, 1 + c0:1 + c1], in_=banks[b][:, 0:n], func=AF.Copy),
                             reads=[Rb[b]], writes=[R_hf[hs]])
                    hf = hfull[hs]
                    ac = accb[hs]
                    P.op("pool", lambda e: e.tensor_scalar(out=ac, in0=hf[:, 1:1025], scalar1=cw[:, blk, 1:2], scalar2=cw[:, blk, 3:4],
                                                           op0=ALU.mult, op1=ALU.add), reads=[R_hf[hs], R_const], writes=[R_acc[hs]])
                    P.op("dve", lambda e: e.scalar_tensor_tensor(out=ac, in0=hf[:, 0:1024], scalar=cw[:, blk, 0:1], in1=ac, op0=ALU.mult,
                                                                 op1=ALU.add), reads=[R_hf[hs], R_const], writes=[R_acc[hs]])
                    P.op("dve", lambda e: e.scalar_tensor_tensor(out=ac, in0=hf[:, 2:1026], scalar=cw[:, blk, 2:3], in1=ac, op0=ALU.mult,
                                                                 op1=ALU.add), reads=[R_hf[hs], R_const], writes=[R_acc[hs]])
                    if sel == 0:
                        P.op("act", lambda e: e.activation(out=sg[:, sub, :], in_=ac, func=AF.Silu), reads=[R_acc[hs]], writes=[R_sg[sub]])
                    else:
                        lc = jl * 2 + sub
                        P.op("dve", lambda e: e.tensor_tensor(out=actT[:, lc, :], in0=ac, in1=sg[:, sub, :], op=ALU.mult),
                             reads=[R_acc[hs], R_sg[sub]], writes=[R_act])
            return fn

        for grp in groups:
            for jl, j in enumerate(grp):
                for sel in range(2):
                    spec = (w_up_d, 0, 16, sel * DFF + j * 256, 256)
                    add_task(ffn_up_task(j, jl, sel, spec), spec)
            nk = 2 * len(grp)
            k0 = grp[0] * 2
            for cb in range(4):
                linear_tm(w_down_d, k0, nk, cb * 512, range(8), lambda kc, t: (actT[:, kc, t * 128:(t + 1) * 128], R_act),
                          resid_add_epi(cb), [6, 7])
        barrier_task()

        P.skip = upto < 8
        R_wpu = Reg()
        R_pT = Reg()
        R_pt = Reg()

        @section
        def sec_p8a():
            P.dma("pool", lambda e: e.dma_start(out=wpu, in_=w_pleup_d.rearrange("(k p) n -> p k n", p=128)), DS_wpu, writes=[R_wpu])
            load_grep(2)
            def p_post(t):
                def f():
                    P.dma("sp", lambda e: e.dma_start(out=ptile, in_=pp_d[t * 128:(t + 1) * 128, :]), DS_pt, writes=[R_pt])
                    P.op("dve", lambda e: e.tensor_copy(out=pbt, in_=ptile), reads=[R_pt], writes=[R_pt])
                    for kc in range(2):
                        P.op("pe", lambda e: e.transpose(out=bankbf[4][:, kc * 128:(kc + 1) * 128], in_=pbt[:, kc * 128:(kc + 1) * 128],
                                                         identity=identb), reads=[R_pt, R_const], writes=[Rb[4]], inc=(kc == 1))
                    P.op("dve", lambda e: e.tensor_copy(out=pT[:, :, t * 128:(t + 1) * 128],
                                                        in_=bankbf[4][:, 0:256].rearrange("p (k c) -> p k c", k=2)),
                         reads=[Rb[4]], writes=[R_pT])
                return f
            norm_pipeline([((lambda t=t: (resid[:, t, :], R_resid[t])), hnT, t * 128, R_hnT[t], p_post(t)) for t in range(8)])
        barrier_task()

        def gate_epi(cb):
            def epi(t, b):
                pb = 4 + (tmp_i[0] % 4)
                s = tmp_i[0] % 2
                tmp_i[0] += 1
                for kc in range(2):
                    P.op("pe", lambda e: e.matmul(banks[pb][:, :], lhsT=pT[:, kc, t * 128:(t + 1) * 128],
                                                  rhs=wpu[:, kc, cb * 512:(cb + 1) * 512], start=(kc == 0), stop=(kc == 1)),
                         reads=[R_pT, R_wpu], writes=[Rb[pb]], inc=(kc == 1))
                t1 = tmpf[2 + s]
                P.op("act", lambda e: e.activation(out=t1, in_=banks[b][:, :], func=AF.Sigmoid), reads=[Rb[b]], writes=[R_tmp[2 + s]])
                P.op("dve", lambda e: e.tensor_tensor(out=t1, in0=banks[pb][:, :], in1=t1, op=ALU.mult), reads=[Rb[pb]],
                     writes=[R_tmp[2 + s]])
                dst = resid[:, t, cb * 512:(cb + 1) * 512]
                P.op("pool", lambda e: e.tensor_tensor(out=dst, in0=dst, in1=t1, op=ALU.add), reads=[R_tmp[2 + s]],
                     writes=[R_resid[t]])
            return epi

        for cb in range(4):
            linear_tm(w_gate_d, 0, 16, cb * 512, range(8), lambda kc, t: (hnT[:, kc, t * 128:(t + 1) * 128], R_hnT[t]),
                      gate_epi(cb), lin_banks)
        barrier_task()

        P.skip = upto < 9
        R_OT = [Reg(), Reg()]

        @section
        def sec_p9():
            P.dma("sp", lambda e: e.dma_start(out=gfin, in_=gfin_d), DS_misc, writes=[R_const])
            for t in range(8):
                s = t % 2
                R_s = Reg()
                ss = stcol()
                src = resid[:, t, :]
                P.op("dve", lambda e: e.scalar_tensor_tensor(out=OT[s], in0=src, scalar=1.0, in1=src, op0=ALU.mult, op1=ALU.mult,
                                                             accum_out=ss), reads=[R_resid[t]], writes=[R_OT[s], R_s])
                r = rsqrt_col(ss, 1.0 / D, R_s)
                P.op("dve", lambda e: e.scalar_tensor_tensor(out=OT[s], in0=src, scalar=r, in1=gfin, op0=ALU.mult, op1=ALU.mult),
                     reads=[R_resid[t], R_s, R_const], writes=[R_OT[s]])
                P.dma("sp", lambda e: e.dma_start(out=y_d[t * 128:(t + 1) * 128, :], in_=OT[s]), DS_o[s], reads=[R_OT[s]])
        P.skip = False
        barrier_task()

        wt_idx = [i for i, tk in enumerate(tasks) if tk[0] is not None]
        q_of = {}
        live = []
        rp = 0
        issued = 0

        def try_issue():
            nonlocal_rp = rp_box[0]
            k = wt_idx[issued_box[0]]
            nq = wquarters(tasks[k][0])
            start = nonlocal_rp
            if nq == 2 and start % 2 == 1:
                start += 1
            oldest = live[0][1] if live else start
            if start + nq - oldest > 4:
                return False
            P.skip = tasks[k][2]
            q_of[k] = start % 4
            issue_weights(tasks[k][0], q_of[k])
            live.append((k, start))
            rp_box[0] = start + nq
            issued_box[0] += 1
            return True

        rp_box = [0]
        issued_box = [0]
        for i, (spec, fn, skip) in enumerate(tasks):
            while issued_box[0] < len(wt_idx) and (wt_idx[issued_box[0]] <= i or True):
                if not try_issue():
                    break
            if spec is not None:
                assert i in q_of, i
            P.skip = skip
            fn(q_of.get(i))
            if spec is not None:
                assert live and live[0][0] == i
                live.pop(0)
        P.skip = False

        for n_, e_ in P.E.items():
            assert not e_.pending, n_
        with nc.Block() as block:
            @block.tensor
            def _(eng):
                P.replay("pe", eng)

            @block.scalar
            def _(eng):
                P.replay("act", eng)

            @block.vector
            def _(eng):
                P.replay("dve", eng)

            @block.gpsimd
            def _(eng):
                P.replay("pool", eng)

            @block.sync
            def _(eng):
                P.replay("sp", eng)
    return nc


def make_in_maps(inp):
    x = np.asarray(inp["x"], np.float32)
    p = np.asarray(inp["p"], np.float32)[0]
    pos = np.asarray(inp["positions"], np.int32)
    sq = lambda k: np.ascontiguousarray(np.asarray(inp[k], np.float32)[0])
    w_in, w_out, w_up, w_down = sq("w_in"), sq("w_out"), sq("w_up"), sq("w_down")
    w_gate, w_pleup = sq("w_ple_gate"), sq("w_ple_up")
    rep = lambda v: np.ascontiguousarray(np.broadcast_to(v, (128,) + v.shape))
    cols = lambda v: np.ascontiguousarray(v.reshape(-1, 128).T)
    gcols = np.ascontiguousarray(np.stack([cols(sq("g_mix")), cols(sq("g_ffn")), cols(sq("g_ple"))], axis=1))
    gfin = rep(np.asarray(inp["g_final"], np.float32))
    grep = np.ascontiguousarray(np.stack([rep(sq("g_mix")), rep(sq("g_ffn")), rep(sq("g_ple"))]))
    lnrep = rep(np.stack([sq("gmlp_ln_g"), sq("gmlp_ln_b")]))
    gsub = rep(sq("g_subln"))
    lamv = rep(np.stack([sq("lambda_q1"), sq("lambda_k1"), sq("lambda_q2"), sq("lambda_k2")]))
    invf = rep((500000.0 ** (-np.arange(0, 16, 2, dtype=np.float32) / 16)).astype(np.float32))
    ws = sq("w_spatial")
    bs = sq("b_spatial")
    cwv = sq("conv_w")
    cbv = sq("conv_b")
    maps = []
    for c in range(8):
        b, hf = c // 2, c % 2
        if hf == 0:
            xl, pl, posl, wsl, bsl, taps = x[b], p[b, :NOWN], pos[b], ws, bs, (0, 1, 2)
        else:
            xl, pl, posl = x[b, ::-1], p[b, ::-1][:NOWN], pos[b, ::-1]
            wsl, bsl, taps = ws[:, ::-1, ::-1], bs[:, ::-1], (2, 1, 0)
        wsT = np.ascontiguousarray(np.transpose(wsl, (2, 0, 1)))
        bsT = np.ascontiguousarray(bsl.T)
        cw4 = np.stack([cwv[taps[0]], cwv[taps[1]], cwv[taps[2]], cbv], axis=-1)
        cw4 = np.ascontiguousarray(cw4.reshape(88, 128, 4).transpose(1, 0, 2))
        maps.append({
            "x": np.ascontiguousarray(xl), "pp": np.ascontiguousarray(pl),
            "pos": np.ascontiguousarray(posl.reshape(16, 128).T.astype(np.int32)), "invf": invf,
            "gcols": gcols, "gfin": gfin, "grep": grep, "lnrep": lnrep, "gsub": gsub, "lamv": lamv, "wsT": wsT, "bsT": bsT, "cw": cw4,
            "w_in": w_in, "w_out": w_out, "w_up": w_up, "w_down": w_down, "w_gate": w_gate, "w_pleup": w_pleup,
        })
    return maps


def assemble(results):
    out = np.empty((4, S, D), np.float32)
    for c in range(8):
        b, hf = c // 2, c % 2
        y = np.asarray(results[c]["y"], np.float32)
        if hf == 0:
            out[b, :NOWN] = y
        else:
            out[b, NOWN:] = y[::-1]
    return out


def kernel(**inputs):
    nc = build_nc()
    maps = make_in_maps(inputs)
    res = run_bass_kernel_spmd(nc, maps, core_ids=list(range(8)))
    return assemble(res.results)
```
